# Optimizing a Trainium2 kernel written in Bass

```python
import jax, jax.numpy as jnp
from jax import lax
import numpy as np

D_MODEL = 1024
BATCH = 8
SEQ = 2048
DEPTH = 1

MIX_WIDTH = D_MODEL
POOL_WIDTH = MIX_WIDTH // 2
POOL_WINDOWS = (2, 4, 8, 16)
POOL_GROUP = POOL_WIDTH // 4
HEAD_DIM = 64
N_Q_HEADS = (MIX_WIDTH - POOL_WIDTH) // HEAD_DIM
N_KV_HEADS = 2
Q_PER_KV = N_Q_HEADS // N_KV_HEADS
IDX_HEADS = 8
IDX_DIM = 32
TOPK_MAX = 256
Q_BLOCK = 128
ROPE_THETA = 10000.0
D_FF = 2816
CONV_WIDTH = 3
EPS = 1e-6
NEG = -1e30

IN_SPLITS = (POOL_WIDTH, N_Q_HEADS * HEAD_DIM, N_KV_HEADS * HEAD_DIM, N_KV_HEADS * HEAD_DIM,
             IDX_HEADS * IDX_DIM, IDX_DIM, IDX_HEADS)
IN_WIDTH = POOL_WIDTH + N_Q_HEADS * HEAD_DIM + 2 * N_KV_HEADS * HEAD_DIM + IDX_HEADS * IDX_DIM + IDX_DIM + IDX_HEADS

kernel_name = "hybrid_pool_dsa_convffn"


def rms_norm(x, g):
    xf = x.astype(jnp.float32)
    y = xf * lax.rsqrt(jnp.mean(xf * xf, axis=-1, keepdims=True) + EPS)
    return (y * g.astype(jnp.float32)).astype(x.dtype)


def rope(x, pos):
    d = x.shape[-1]
    half = d // 2
    inv = jnp.exp(-jnp.log(jnp.float32(ROPE_THETA)) * jnp.arange(half, dtype=jnp.float32) / half)
    ang = pos.astype(jnp.float32)[:, None] * inv[None, :]
    cos = jnp.cos(ang)[:, None, :]
    sin = jnp.sin(ang)[:, None, :]
    xf = x.astype(jnp.float32)
    x1, x2 = xf[..., :half], xf[..., half:]
    out = jnp.concatenate([x1 * cos - x2 * sin, x2 * cos + x1 * sin], axis=-1)
    return out.astype(x.dtype)


def pool_mixer(v, pool_w, pool_scale):
    B, T, C = v.shape
    vf = v.astype(jnp.float32)
    csum = jnp.cumsum(vf, axis=1)
    t = jnp.arange(T)
    outs = []
    for g, w in enumerate(POOL_WINDOWS):
        sl = slice(g * POOL_GROUP, (g + 1) * POOL_GROUP)
        cg = csum[..., sl]
        lag = jnp.pad(cg, ((0, 0), (w, 0), (0, 0)))[:, :T]
        cnt = jnp.minimum(t + 1, w).astype(jnp.float32)[None, :, None]
        outs.append((cg - lag) / cnt - vf[..., sl])
    p = jnp.stack(outs, axis=2)
    y = jnp.einsum('btgc,gcd->btgd', p, pool_w.astype(jnp.float32)).reshape(B, T, C)
    return (y * pool_scale.astype(jnp.float32)).astype(v.dtype)


def dsa_mixer(q, k, v, qi, ki, wi):
    B, T = q.shape[0], q.shape[1]
    topk = min(TOPK_MAX, T // 4)
    nb = T // Q_BLOCK
    kpos = jnp.arange(T)
    b_idx = jnp.arange(B)[:, None, None]
    kif = ki.astype(jnp.float32)
    scale = HEAD_DIM ** -0.5

    def to_blocks(a):
        return a.reshape((a.shape[0], nb, Q_BLOCK) + a.shape[2:]).swapaxes(0, 1)

    def block(args):
        qb, qib, wib, t0 = args
        qpos = t0 + jnp.arange(Q_BLOCK)
        causal = kpos[None, :] <= qpos[:, None]
        rel = jax.nn.relu(jnp.einsum('bqhd,bsd->bqsh', qib.astype(jnp.float32), kif))
        score = jnp.einsum('bqsh,bqh->bqs', rel, wib.astype(jnp.float32))
        score = jnp.where(causal[None], score, NEG)
        _, idx = lax.top_k(score, topk)
        kg = k[b_idx, idx].astype(jnp.float32)
        vg = v[b_idx, idx].astype(jnp.float32)
        qg = qb.reshape(B, Q_BLOCK, N_KV_HEADS, Q_PER_KV, HEAD_DIM).astype(jnp.float32)
        logits = jnp.einsum('bqjgd,bqnjd->bqjgn', qg, kg) * scale
        valid = (idx <= qpos[None, :, None])[:, :, None, None, :]
        probs = jax.nn.softmax(jnp.where(valid, logits, NEG), axis=-1)
        o = jnp.einsum('bqjgn,bqnjd->bqjgd', probs, vg)
        return o.reshape(B, Q_BLOCK, N_Q_HEADS * HEAD_DIM).astype(q.dtype)

    outs = lax.map(block, (to_blocks(q), to_blocks(qi), to_blocks(wi), jnp.arange(nb) * Q_BLOCK))
    return outs.swapaxes(0, 1).reshape(B, T, N_Q_HEADS * HEAD_DIM)


def conv_ffn(h, w_up, conv_w, conv_b, w_down):
    u = h @ w_up
    T = u.shape[1]
    up = jnp.pad(u, ((0, 0), (CONV_WIDTH - 1, 0), (0, 0)))
    c = conv_b
    for j in range(CONV_WIDTH):
        c = c + up[:, j:j + T] * conv_w[j]
    gate, val = jnp.split(c, 2, axis=-1)
    return (jax.nn.silu(gate) * val) @ w_down


def setup_inputs(seed: int = 0) -> dict:
    key = jax.random.key(seed)
    ks = jax.random.split(key, 13)
    f32 = jnp.float32
    n = lambda k, s: jax.random.normal(k, s, dtype=f32)
    return {
        "x": n(ks[0], (BATCH, SEQ, D_MODEL)),
        "norm1_g": 1.0 + 0.02 * n(ks[1], (DEPTH, D_MODEL)),
        "w_in": n(ks[2], (DEPTH, D_MODEL, IN_WIDTH)) * D_MODEL ** -0.5,
        "q_norm_g": 1.0 + 0.02 * n(ks[3], (DEPTH, HEAD_DIM)),
        "k_norm_g": 1.0 + 0.02 * n(ks[4], (DEPTH, HEAD_DIM)),
        "pool_w": n(ks[5], (DEPTH, 4, POOL_GROUP, POOL_GROUP)) * POOL_GROUP ** -0.5,
        "pool_scale": 1.0 + 0.02 * n(ks[6], (DEPTH, POOL_WIDTH)),
        "w_out": n(ks[7], (DEPTH, MIX_WIDTH, D_MODEL)) * MIX_WIDTH ** -0.5,
        "norm2_g": 1.0 + 0.02 * n(ks[8], (DEPTH, D_MODEL)),
        "w_up": n(ks[9], (DEPTH, D_MODEL, 2 * D_FF)) * D_MODEL ** -0.5,
        "conv_w": n(ks[10], (DEPTH, CONV_WIDTH, 2 * D_FF)) * CONV_WIDTH ** -0.5,
        "conv_b": 0.02 * n(ks[11], (DEPTH, 2 * D_FF)),
        "w_down": n(ks[12], (DEPTH, D_FF, D_MODEL)) * D_FF ** -0.5,
    }


def reference(x, norm1_g, w_in, q_norm_g, k_norm_g, pool_w, pool_scale, w_out,
              norm2_g, w_up, conv_w, conv_b, w_down):
    B, T, _ = x.shape
    pos = jnp.arange(T)
    cuts = [int(c) for c in np.cumsum(IN_SPLITS)[:-1]]
    for l in range(DEPTH):
        h = rms_norm(x, norm1_g[l])
        proj = h @ w_in[l]
        v_pool, q, k, v, qi, ki, wi = jnp.split(proj, cuts, axis=-1)
        q = rope(rms_norm(q.reshape(B, T, N_Q_HEADS, HEAD_DIM), q_norm_g[l]), pos)
        k = rope(rms_norm(k.reshape(B, T, N_KV_HEADS, HEAD_DIM), k_norm_g[l]), pos)
        v = v.reshape(B, T, N_KV_HEADS, HEAD_DIM)
        qi = rope(qi.reshape(B, T, IDX_HEADS, IDX_DIM), pos)
        ki = rope(ki.reshape(B, T, 1, IDX_DIM), pos)[:, :, 0]
        wi = wi * (IDX_HEADS ** -0.5 * IDX_DIM ** -0.5)
        a_out = pool_mixer(v_pool, pool_w[l], pool_scale[l])
        b_out = dsa_mixer(q, k, v, qi, ki, wi)
        x = x + jnp.concatenate([a_out, b_out], axis=-1) @ w_out[l]
        x = x + conv_ffn(rms_norm(x, norm2_g[l]), w_up[l], conv_w[l], conv_b[l], w_down[l])
    return x
```

```python
import math
from contextlib import ExitStack

import numpy as np
import concourse.bass as bass
import concourse.mybir as mybir
from concourse.bass_utils import run_bass_kernel_spmd

F32 = mybir.dt.float32
BF16 = mybir.dt.bfloat16
U32 = mybir.dt.uint32
ALU = mybir.AluOpType
AF = mybir.ActivationFunctionType
AX = mybir.AxisListType

T = 2048
D = 1024
NCH = 4
CH = 512
KC = 8
DFF = 2816
NFT = 22
EPS = 1e-6
TOPK = 256
NITER = 16
NEG = -1e30
SB_BASE = 16512
SB_END = 229344


class Buf:
    __slots__ = ("name", "w", "r", "excl")

    def __init__(self, name="", excl=False):
        self.name = name
        self.w = None
        self.r = []
        self.excl = excl


class Eng:
    def __init__(self, name, idx):
        self.name = name
        self.idx = idx
        self.sem = None
        self.count = 0
        self.known = {}
        self.prog = []


class FW:
    ENG_NAMES = ["pe", "act", "dve", "pool", "sp"]

    def __init__(self, nc, n_dma_sems=16):
        self.nc = nc
        self.E = {n: Eng(n, i) for i, n in enumerate(self.ENG_NAMES)}
        self.by_idx = {e.idx: e for e in self.E.values()}
        self.qsems = {"sp": list(range(0, 12)), "pool": list(range(12, 20)), "act": list(range(20, 24))}
        n_dma_sems = 24
        self.n_dma_sems = n_dma_sems
        self.dma_n = {"sp": 0, "pool": 0, "act": 0}
        self.dma_last_val = [0] * n_dma_sems
        self.n_ins = 0
        self.pool_dmas = []
        self.max_pool_outstanding = 2

    def setup(self, stack):
        nc = self.nc
        for n, e in self.E.items():
            e.sem = stack.enter_context(nc.semaphore("s_" + n))
        self.dma_sems = [stack.enter_context(nc.semaphore("s_dma%d" % i))
                         for i in range(self.n_dma_sems)]

    def _need_wait(self, E, tok, raw):
        kind = tok[0]
        if kind == "e":
            _, idx, c = tok
            if idx == E.idx:
                if E.name == "pe":
                    return None
            key = ("e", idx)
            if E.known.get(key, 0) >= c:
                return None
            E.known[key] = c
            return (self.by_idx[idx].sem, c)
        _, si, v = tok
        key = ("d", si)
        if E.known.get(key, 0) >= v:
            return None
        E.known[key] = v
        return (self.dma_sems[si], v)

    def _waits(self, E, reads, writes, extra):
        deps = []
        for b in reads:
            if b.w is not None:
                deps.append((b.w, True))
        for b in writes:
            if b.w is not None:
                deps.append((b.w, False))
            for t in b.r:
                deps.append((t, False))
        for t in extra:
            deps.append((t, True))
        waits = []
        for (d, raw) in deps:
            w = self._need_wait(E, d, raw)
            if w is not None:
                waits.append(w)
        return waits

    def _record(self, tok, reads, writes):
        for b in reads:
            b.r = [t for t in b.r if not (t[0] == tok[0] and t[1] == tok[1])]
            b.r.append(tok)
        for b in writes:
            b.w = tok
            b.r = []

    def op(self, eng, build, reads=(), writes=(), signal=True, extra=()):
        E = self.E[eng]
        if any(b.excl for b in reads):
            writes = list(writes) + [b for b in reads if b.excl]
            reads = [b for b in reads if not b.excl]
        waits = self._waits(E, reads, writes, extra)
        self.n_ins += 1
        if signal:
            E.count += 1
            tok = ("e", E.idx, E.count)
            sem = E.sem

            def f(h, waits=waits, build=build, sem=sem):
                for (s, v) in waits:
                    h.wait_ge(s, v)
                build(h).then_inc(sem, 1)
        else:
            tok = ("e", E.idx, E.count + 1)

            def f(h, waits=waits, build=build):
                for (s, v) in waits:
                    h.wait_ge(s, v)
                build(h)
        E.prog.append(f)
        self._record(tok, reads, writes)
        return tok

    def dma(self, queue, out_ap, in_ap, reads=(), writes=(), extra=()):
        E = self.E[queue]
        qs = self.qsems[queue]
        si = qs[self.dma_n[queue] % len(qs)]
        self.dma_n[queue] += 1
        prev = self.dma_last_val[si]
        val = prev + 16
        self.dma_last_val[si] = val
        tok = ("d", si, val)
        ex = list(extra)
        if prev > 0:
            ex.append(("d", si, prev))
        if queue == "pool":
            if len(self.pool_dmas) >= self.max_pool_outstanding:
                ex.append(self.pool_dmas[-self.max_pool_outstanding])
            self.pool_dmas.append(tok)
        waits = self._waits(E, reads, writes, ex)
        sem = self.dma_sems[si]

        def f(h, waits=waits, sem=sem, out_ap=out_ap, in_ap=in_ap):
            for (s, v) in waits:
                h.wait_ge(s, v)
            h.dma_start(out=out_ap, in_=in_ap).then_inc(sem, 16)
        E.prog.append(f)
        self._record(tok, reads, writes)
        return tok

    def all_tokens(self):
        toks = []
        for e in self.E.values():
            if e.count > 0:
                toks.append(("e", e.idx, e.count))
        for si, v in enumerate(self.dma_last_val):
            if v > 0:
                toks.append(("d", si, v))
        return toks

    def wait_all(self, eng, toks):
        E = self.E[eng]
        waits = []
        for d in toks:
            w = self._need_wait(E, d, True)
            if w is not None:
                waits.append(w)
        if not waits:
            return

        def f(h, waits=waits):
            for (s, v) in waits:
                h.wait_ge(s, v)
        E.prog.append(f)

    def barrier(self):
        toks = self.all_tokens()
        for n in self.ENG_NAMES:
            self.wait_all(n, toks)

    def finish(self):
        nc = self.nc
        fw = self
        with nc.Block() as block:
            @block.tensor
            def _(h):
                for f in fw.E["pe"].prog:
                    f(h)

            @block.scalar
            def _(h):
                for f in fw.E["act"].prog:
                    f(h)

            @block.vector
            def _(h):
                for f in fw.E["dve"].prog:
                    f(h)

            @block.gpsimd
            def _(h):
                for f in fw.E["pool"].prog:
                    f(h)

            @block.sync
            def _(h):
                for f in fw.E["sp"].prog:
                    f(h)


def _cst_layout():
    lay = {}
    off = 0
    for name, n in [("g1", 8), ("g2", 8), ("gq", 1), ("gqp", 1), ("gk", 1), ("gkp", 1),
                    ("ps", 4), ("cw", 132), ("cb", 44), ("invc", 64), ("pw2", NITER + 2),
                    ("cmask", 128), ("sel", 64), ("eps", 1), ("zero", 1)]:
        lay[name] = (off, n)
        off += n
    return lay, off


CST, NCST = _cst_layout()
NWIN = 22


class K:
    def __init__(self, debug=None):
        self.debug = debug or []
        self.nc = bass.Bass("TRN2", target_bir_lowering=False)
        self.fw = FW(self.nc)
        self.sb_off = SB_BASE
        self.top_off = SB_END
        self.n_t = 0

    def sb(self, name, shape, dt):
        esz = 4 if dt in (F32, U32) else 2
        sz = int(np.prod(shape[1:])) * esz
        sz = (sz + 63) // 64 * 64
        assert self.sb_off + sz <= self.top_off, (name, self.sb_off, sz)
        self.n_t += 1
        t = self.nc.alloc_sbuf_tensor_at("%s_%d" % (name, self.n_t), list(shape), dt, offset=self.sb_off)
        self.sb_off += sz
        return t

    def sbtop(self, name, shape, dt):
        esz = 4 if dt in (F32, U32) else 2
        sz = int(np.prod(shape[1:])) * esz
        sz = (sz + 63) // 64 * 64
        self.top_off -= sz
        self.top_off = self.top_off // 64 * 64
        assert self.top_off >= self.sb_off, (name, self.top_off, self.sb_off)
        self.n_t += 1
        return self.nc.alloc_sbuf_tensor_at("%s_%d" % (name, self.n_t), list(shape), dt, offset=self.top_off)

    def mark(self):
        return self.sb_off

    def release(self, m):
        self.fw.barrier()
        self.sb_off = m

    def mm(self, out, lhsT, rhs, start, stop, reads, writes, signal=True, tp=None):
        if tp is None:
            return self.fw.op("pe", lambda h: h.matmul(out, lhsT, rhs, start=start, stop=stop),
                              reads, writes, signal)
        return self.fw.op("pe", lambda h: h.matmul(out, lhsT, rhs, start=start, stop=stop, tile_position=tp),
                          reads, writes, signal)

    def tr(self, out, in_, ident, reads, writes, signal=True):
        return self.fw.op("pe", lambda h: h.transpose(out, in_, ident), reads, writes, signal)

    def act(self, out, in_, func, reads, writes, scale=1.0, bias=None, accum_out=None):
        kw = {}
        if bias is not None:
            kw["bias"] = bias
        if accum_out is not None:
            kw["accum_out"] = accum_out
        return self.fw.op("act", lambda h: h.activation(out, in_, func, scale=scale, **kw), reads, writes)

    def tt(self, eng, out, in0, in1, op, reads, writes):
        return self.fw.op(eng, lambda h: h.tensor_tensor(out, in0, in1, op), reads, writes)

    def ts(self, eng, out, in0, s1, s2, op0, op1, reads, writes, accum_out=None):
        if op1 is None:
            return self.fw.op(eng, lambda h: h.tensor_scalar(out, in0, s1, None, op0), reads, writes)
        if accum_out is None:
            return self.fw.op(eng, lambda h: h.tensor_scalar(out, in0, s1, s2, op0, op1), reads, writes)
        return self.fw.op(eng, lambda h: h.tensor_scalar(out, in0, s1, s2, op0, op1, accum_out=accum_out),
                          reads, writes)

    def stt(self, out, in0, scalar, in1, op0, op1, reads, writes):
        return self.fw.op("dve", lambda h: h.scalar_tensor_tensor(out, in0, scalar, in1, op0, op1), reads, writes)

    def copy(self, eng, out, in_, reads, writes):
        if eng == "act":
            return self.fw.op("act", lambda h: h.activation(out, in_, AF.Copy), reads, writes)
        return self.fw.op(eng, lambda h: h.tensor_copy(out, in_), reads, writes)

    def recip(self, out, in_, reads, writes):
        return self.fw.op("dve", lambda h: h.reciprocal(out, in_), reads, writes)

    def memset(self, eng, ap, val, writes):
        return self.fw.op(eng, lambda h: h.memset(ap, val), (), writes)

    def cst(self, name, a=0, n=None):
        o, w = CST[name]
        if n is None:
            n = w - a
        return self.cst_t[:, o + a:o + a + n]

    def build(self, stop_after=None):
        nc = self.nc
        fw = self.fw
        dram = {}

        def din(name, shape):
            dram[name] = nc.dram_tensor(name, list(shape), F32, kind="ExternalInput").ap()
            return dram[name]

        xT = din("xT", [128, T // 256, KC, 256])
        cst_d = din("cst", [128, NCST])
        cstb_d = din("cstb", [128, 384])
        csq_d = din("csq", [128, 2, T])
        csi_d = din("csi", [128, 2, T])
        win_d = din("win", [NWIN, 128, KC * 128])
        wvw_d = din("wvw", [128, KC * 136])
        poolw_d = din("poolw", [128, 4 * 128])
        wout_d = din("wout", [128, 8 * D])
        wup_d = din("wup", [NFT, 128, 2 * KC * 128])
        wdn_d = din("wdn", [8, 128, NFT * 128])
        outT = nc.dram_tensor("outT", [128, KC, T], F32, kind="ExternalOutput").ap()
        dbg = {}
        self.dbg = dbg

        def ddbg(name, shape):
            dbg[name] = nc.dram_tensor("dbg_" + name, list(shape), F32, kind="ExternalOutput").ap()
            return dbg[name]

        with ExitStack() as st:
            fw.setup(st)
            ps = [st.enter_context(nc.psum_tensor("ps%d" % i, [128, 512], F32)) for i in range(8)]
            psB = [Buf("ps%d" % i, excl=True) for i in range(8)]
            psT = [ps[4][:, :].bitcast(BF16)]
            psTB = [psB[4]]
            self.ps, self.psB, self.psT, self.psTB = ps, psB, psT, psTB

            self.cst_t = self.sb("cst", [128, NCST], F32)
            cstb = self.sb("cstb", [128, 384], BF16)
            B_cst = Buf("cst")
            B_cstb = Buf("cstb")
            fw.dma("sp", self.cst_t[:], cst_d, writes=[B_cst])
            fw.dma("pool", cstb[:], cstb_d, writes=[B_cstb])
            ident = cstb[:, 0:128]
            ones = cstb[:, 128:256]
            blockones = cstb[:, 256:384]
            eps_ap = self.cst("eps")
            self.B_cst, self.B_cstb = B_cst, B_cstb

            m_const = self.mark()
            off_mixT = self.sb_off
            mixT = self.sb("mixT", [128, 8, T], BF16)
            mixB = [[Buf("mix%d_%d" % (s, c)) for c in range(NCH)] for s in range(8)]
            off_wout = self.sb_off
            wout_sb = self.sb("wout", [128, 8, D], BF16)
            B_wout = Buf("wout")
            m_persist = self.mark()

            qT = self.sb("qT", [128, 4, T], BF16)
            kTz = self.sb("kTz", [128, 2, 2, T], BF16)
            vxe = self.sb("vxe", [128, 16, 2, 128], BF16)
            vxo = self.sb("vxo", [128, 16, 2, 128], BF16)
            qiT = self.sb("qiT", [128, 2, T], BF16)
            kiT = self.sb("kiT", [128, T], BF16)
            wts = self.sb("wts", [128, 16, 8], F32)
            qB = [Buf("q%d" % c) for c in range(NCH)]
            kB = [Buf("k%d" % c) for c in range(NCH)]
            vB = [Buf("v%d" % c) for c in range(NCH)]
            qiB = [Buf("qi%d" % c) for c in range(NCH)]
            kiB = [Buf("ki%d" % c) for c in range(NCH)]
            wB = [Buf("w%d" % c) for c in range(NCH)]
            m_ab = self.mark()

            xnT = self.sb("xnT", [128, KC, T], BF16)
            xnB = [Buf("xn%d" % c) for c in range(NCH)]
            m_a = self.mark()
            wpl = self.sb("wpl", [128, 4, KC, 128], BF16)
            wplB = [Buf("wpl%d" % g) for g in range(4)]
            poolw = self.sb("poolw", [128, 4, 128], BF16)
            poolwB = Buf("poolw")
            for g in range(4):
                fw.dma("pool", wpl[:, g].rearrange("p k n -> p (k n)"), win_d[g], writes=[wplB[g]])
            fw.dma("pool", poolw[:].rearrange("p g d -> p (g d)"), poolw_d, writes=[poolwB])
            m_a2 = self.mark()

            NB = 4
            CH2 = 256
            xc = [self.sb("xc%d" % i, [128, KC, CH2], F32) for i in range(NB)]
            sqb = [self.sb("sqb%d" % i, [128, KC, CH2], BF16) for i in range(NB)]
            sr = [self.sb("sr%d" % i, [128, CH2], F32) for i in range(NB)]
            rs = sr
            xcB = [Buf() for _ in range(NB)]
            sqB = [Buf() for _ in range(NB)]
            srB = [Buf() for _ in range(NB)]
            rsB = srB
            def x_dma(c2):
                fw.dma("sp" if c2 % 2 == 0 else "act", xc[c2 % NB][:], xT[:, c2], writes=[xcB[c2 % NB]])
            for c2 in range(NB):
                x_dma(c2)
            for c2 in range(T // CH2):
                i = c2 % NB
                c = (c2 * CH2) // CH
                cs = slice(c2 * CH2, (c2 + 1) * CH2)
                self.act(sqb[i][:], xc[i][:], AF.Square, [xcB[i]], [sqB[i]])
                pb = c2 % 4
                for kc in range(KC):
                    self.mm(ps[pb][:, 0:CH2], ones, sqb[i][:, kc, :], kc == 0, kc == KC - 1,
                            [sqB[i], B_cstb], [psB[pb]], signal=(kc == KC - 1))
                self.act(sr[i][:], ps[pb][:, 0:CH2], AF.Ln, [psB[pb], B_cst], [srB[i]], scale=1.0 / D, bias=eps_ap)
                self.act(rs[i][:], sr[i][:], AF.Exp, [srB[i]], [rsB[i]], scale=-0.5)
                for kc in range(KC):
                    self.stt(xnT[:, kc, cs], xc[i][:, kc, :], self.cst("g1", kc, 1), rs[i][:],
                             ALU.mult, ALU.mult, [xcB[i], rsB[i], B_cst], [xnB[c]])
                nxt = c2 + NB
                if nxt < T // CH2 and nxt % 2 == 0:
                    x_dma(nxt)
                prv = c2 - 1 + NB
                if c2 >= 1 and prv < T // CH2 and prv % 2 == 1:
                    x_dma(prv)
            self.release(m_a2)
            if "xnT" in self.debug:
                d = ddbg("xnT", [128, KC, T])
                self.dump_bf16(d, xnT[:], [128, KC, T], xnB)

            if stop_after == "N1":
                self.finalize_stub(outT)
                return nc
            PADL = 16
            vpT = self.sb("vpT", [128, 4, PADL + T], F32)
            vpB = [Buf("vp%d" % g) for g in range(4)]
            pa = self.sb("pa", [128, PADL + T], F32)
            pbb = self.sb("pb", [128, PADL + T], F32)
            paB, pbB = Buf("pa"), Buf("pb")
            pbf = self.sb("pbf", [128, T], BF16)
            pbfB = Buf("pbf")
            ptmp = self.sb("ptmp", [128, 16], F32)
            ptmpB = Buf("ptmp")
            for g in range(4):
                self.memset("pool", vpT[:, g, 0:PADL], 0.0, [vpB[g]])
            bank = 0
            for g in range(4):
                for c in range(NCH):
                    pb = bank % 4
                    bank += 1
                    for kc in range(KC):
                        self.mm(ps[pb][:, :], wpl[:, g, kc, :], xnT[:, kc, c * CH:(c + 1) * CH], kc == 0, kc == KC - 1,
                                [wplB[g], xnB[c]], [psB[pb]], signal=(kc == KC - 1))
                    self.copy("act", vpT[:, g, PADL + c * CH:PADL + (c + 1) * CH], ps[pb][:, :], [psB[pb]], [vpB[g]])
            L = PADL + T
            self.memset("pool", pa[:, 0:PADL], 0.0, [paB])
            self.memset("pool", pbb[:, 0:PADL], 0.0, [pbB])
            self.n_t += 1
            wq_al = nc.alloc_sbuf_tensor_at("wq_al_%d" % self.n_t, [128, 8, KC, 128], BF16, offset=off_wout)
            self.n_t += 1
            csq_al = nc.alloc_sbuf_tensor_at("csq_al_%d" % self.n_t, [128, 2, T], F32, offset=off_mixT + 4 * T * 2)
            wqB = [Buf("wq%d" % i) for i in range(8)]
            csqB = Buf("csq")
            fw.dma("sp", csq_al[:], csq_d, writes=[csqB])
            for i in (0, 4, 1, 5, 2, 6, 3, 7):
                fw.dma("pool", wq_al[:, i].rearrange("p k n -> p (k n)"), win_d[4 + i], writes=[wqB[i]])
            evac_pending = []
            for g in (0, 1, 2, 3):
                w = (2, 4, 8, 16)[g]
                V = vpT[:, g, :]
                src, srcB = V, vpB[g]
                sh = 1
                weng, bufs = "dve", [(pa, paB), (pbb, pbB)]
                bi = 0
                while sh < w:
                    dst, dstB = bufs[bi]
                    bi ^= 1
                    self.tt(weng, dst[:, sh:L], src[:, sh:L], src[:, 0:L - sh], ALU.add, [srcB], [dstB])
                    src, srcB = dst, dstB
                    sh *= 2
                S = src
                while evac_pending:
                    evac_pending.pop(0)()
                self.stt(pbf[:, :], S[:, PADL:L], 1.0 / w, V[:, PADL:L], ALU.mult, ALU.subtract, [srcB, vpB[g]], [pbfB])
                nfix = w - 1
                self.tt("dve", ptmp[:, 0:nfix], S[:, PADL:PADL + nfix], self.cst("invc", g * 16, nfix), ALU.mult,
                        [srcB, B_cst], [ptmpB])
                self.tt("dve", pbf[:, 0:nfix], ptmp[:, 0:nfix], V[:, PADL:PADL + nfix], ALU.subtract,
                        [ptmpB, vpB[g]], [pbfB])
                for c in range(NCH):
                    pb = bank % 4
                    bank += 1
                    self.mm(ps[pb][:, :], poolw[:, g, :], pbf[:, c * CH:(c + 1) * CH], True, True,
                            [poolwB, pbfB], [psB[pb]])
                    evac_pending.append((lambda g=g, c=c, pb=pb: self.ts(
                        "dve", mixT[:, g, c * CH:(c + 1) * CH], ps[pb][:, :], self.cst("ps", g, 1), None,
                        ALU.mult, None, [psB[pb], B_cst], [mixB[g][c]])))
            while evac_pending:
                evac_pending.pop(0)()
            self.release(m_a)
            if "aout" in self.debug:
                d = ddbg("aout", [128, 4, T])
                self.dump_bf16(d, mixT[:, 0:4, :], [128, 4, T], [mixB[g][c] for g in range(4) for c in range(NCH)])

            if stop_after == "POOL":
                self.finalize_stub(outT)
                return nc
            cstab = self.sb("cstab", [128, 2, T], F32)
            cstabB = Buf("cstab")
            fw.dma("sp", cstab[:], csi_d, writes=[cstabB])
            wk = self.sb("wk", [128, 4, KC, 128], BF16)
            wkB = [Buf("wk%d" % i) for i in range(4)]
            wt = {}
            for i in range(8):
                wt[i] = ((lambda kc, i=i: wq_al[:, i, kc, :]), wqB[i], wq_al[:, i])
            for i in range(4):
                wt[8 + i] = ((lambda kc, i=i: wk[:, i, kc, :]), wkB[i], wk[:, i])
            widx = self.sb("widx", [128, 4, KC, 128], BF16)
            widxB = [Buf("widx%d" % i) for i in range(4)]
            iw = {i: ((lambda kc, i=i: widx[:, i, kc, :]), widxB[i], widx[:, i]) for i in range(4)}
            iw[4] = wt[0]
            iw[5] = wt[1]
            wvw = self.sb("wvw", [128, KC, 136], BF16)
            wvwB = Buf("wvw")
            pend = []
            for i in (8, 10, 9, 11):
                pend.append((lambda i=i: fw.dma("pool", wt[i][2].rearrange("p k n -> p (k n)"), win_d[4 + i],
                                                writes=[wt[i][1]])))
            for i in (0, 2, 1, 3):
                pend.append((lambda i=i: fw.dma("pool", iw[i][2].rearrange("p k n -> p (k n)"), win_d[16 + i],
                                                writes=[iw[i][1]])))
            pend.append(lambda: fw.dma("pool", wvw[:].rearrange("p k n -> p (k n)"), wvw_d, writes=[wvwB]))
            for _ in range(2):
                pend.pop(0)()
            NTB = 2
            t1 = [self.sb("t1_%d" % i, [128, CH], F32) for i in range(NTB)]
            t2 = [self.sb("t2_%d" % i, [128, CH], F32) for i in range(NTB)]
            sq2 = [self.sb("sq2_%d" % i, [128, CH], BF16) for i in range(NTB)]
            s2 = [self.sb("s2_%d" % i, [128, CH], F32) for i in range(NTB)]
            t1B = [Buf() for _ in range(NTB)]
            t2B = [Buf() for _ in range(NTB)]
            sq2B = [Buf() for _ in range(NTB)]
            s2B = [Buf() for _ in range(NTB)]
            r2, r2B = s2, s2B
            t3, t3B = t1, t1B
            it = 0
            groups = []
            for p in range(4):
                groups.append((p, 4 + p, "gq", "gqp", (lambda c, p=p: qT[:, p, c * CH:(c + 1) * CH]), qB))
            for g in range(2):
                groups.append((8 + g, 10 + g, "gk", "gkp", g, kB))
            for g in range(2):
                self.memset("pool", kTz[64:128, g, 0, :], 0.0, kB)
                self.memset("pool", kTz[0:64, g, 1, :], 0.0, kB)
            for c in range(NCH):
                cs = slice(c * CH, (c + 1) * CH)
                for (ri, pi, gn, gpn, dst, dB) in groups:
                    i = it % NTB
                    it += 1
                    pr, pp, pq = 0 + 3 * (it % 2), 1 + 3 * (it % 2), 2 + 3 * (it % 2)
                    for kc in range(KC):
                        self.mm(ps[pr][:, :], wt[ri][0](kc), xnT[:, kc, cs], kc == 0, kc == KC - 1,
                                [wt[ri][1], xnB[c]], [psB[pr]], signal=(kc == KC - 1))
                    for kc in range(KC):
                        self.mm(ps[pp][:, :], wt[pi][0](kc), xnT[:, kc, cs], kc == 0, kc == KC - 1,
                                [wt[pi][1], xnB[c]], [psB[pp]], signal=(kc == KC - 1))
                    self.act(sq2[i][:], ps[pr][:, :], AF.Square, [psB[pr]], [sq2B[i]])
                    self.mm(ps[pq][:, :], blockones, sq2[i][:], True, True, [B_cstb, sq2B[i]], [psB[pq]])
                    self.stt(t1[i][:], ps[pr][:, :], self.cst(gn), csq_al[:, 0, cs], ALU.mult, ALU.mult,
                             [psB[pr], B_cst, csqB], [t1B[i]])
                    self.stt(t2[i][:], ps[pp][:, :], self.cst(gpn), csq_al[:, 1, cs], ALU.mult, ALU.mult,
                             [psB[pp], B_cst, csqB], [t2B[i]])
                    self.tt("dve", t3[i][:], t1[i][:], t2[i][:], ALU.add, [t1B[i], t2B[i]], [t3B[i]])
                    self.act(s2[i][:], ps[pq][:, :], AF.Ln, [psB[pq], B_cst], [s2B[i]], scale=1.0 / 64, bias=eps_ap)
                    self.act(r2[i][:], s2[i][:], AF.Exp, [s2B[i]], [r2B[i]], scale=-0.5)
                    if callable(dst):
                        self.tt("dve", dst(c), t3[i][:], r2[i][:], ALU.mult, [t3B[i], r2B[i]], [dB[c]])
                    else:
                        g_ = dst
                        self.tt("dve", kTz[0:64, g_, 0, cs], t3[i][0:64, :], r2[i][0:64, :], ALU.mult, [t3B[i], r2B[i]], [dB[c]])
                        self.tt("dve", kTz[64:128, g_, 1, cs], t3[i][64:128, :], r2[i][64:128, :], ALU.mult,
                                [t3B[i], r2B[i]], [dB[c]])
                    for _ in range(2):
                        if pend:
                            pend.pop(0)()
            if "qT" in self.debug:
                d = ddbg("qT", [128, 4, T])
                self.dump_bf16(d, qT[:], [128, 4, T], qB)
                d = ddbg("kT2", [128, 4, T])
                self.dump_bf16(d, kTz[:].rearrange("p g r t -> p (g r) t"), [128, 4, T], kB)

            if stop_after == "QK":
                self.finalize_stub(outT)
                return nc
            while pend:
                pend.pop(0)()
            for i in (4, 5):
                fw.dma("pool", iw[i][2].rearrange("p k n -> p (k n)"), win_d[16 + i], writes=[iw[i][1]])
            for c in range(NCH):
                self.memset("pool", vxe[:, 4 * c:4 * c + 4, :, 64:128], 1.0, [vB[c]])
                self.memset("pool", vxo[:, 4 * c:4 * c + 4, :, 0:64], 1.0, [vB[c]])
            for tt_ in range(16):
                c = tt_ // 4
                pb = 6 + (tt_ % 2)
                for kc in range(KC):
                    self.mm(ps[pb][:, 0:136], xnT[:, kc, tt_ * 128:(tt_ + 1) * 128], wvw[:, kc, :], kc == 0, kc == KC - 1,
                            [wvwB, xnB[c]], [psB[pb]], signal=(kc == KC - 1))
                vsrc = ps[pb][:, 0:128].rearrange("p (g d) -> p g d", g=2)
                self.copy("act", vxe[:, tt_, :, 0:64], vsrc, [psB[pb]], [vB[c]])
                self.copy("dve", vxo[:, tt_, :, 64:128], vsrc, [psB[pb]], [vB[c]])
                self.ts("dve", wts[:, tt_, :], ps[pb][:, 128:136], 0.0625, None, ALU.mult, None, [psB[pb]], [wB[c]])
            igroups = [(0, 2, (lambda c: qiT[:, 0, c * CH:(c + 1) * CH]), qiB),
                       (1, 3, (lambda c: qiT[:, 1, c * CH:(c + 1) * CH]), qiB),
                       (4, 5, (lambda c: kiT[:, c * CH:(c + 1) * CH]), kiB)]
            for c in range(NCH):
                cs = slice(c * CH, (c + 1) * CH)
                for (ri, pi, dst, dB) in igroups:
                    i = it % NTB
                    it += 1
                    pr, pp = 0 + 3 * (it % 2), 1 + 3 * (it % 2)
                    for kc in range(KC):
                        self.mm(ps[pr][:, :], iw[ri][0](kc), xnT[:, kc, cs], kc == 0, kc == KC - 1,
                                [iw[ri][1], xnB[c]], [psB[pr]], signal=(kc == KC - 1))
                    for kc in range(KC):
                        self.mm(ps[pp][:, :], iw[pi][0](kc), xnT[:, kc, cs], kc == 0, kc == KC - 1,
                                [iw[pi][1], xnB[c]], [psB[pp]], signal=(kc == KC - 1))
                    self.tt("dve", t1[i][:], ps[pr][:, :], cstab[:, 0, cs], ALU.mult, [psB[pr], cstabB], [t1B[i]])
                    self.tt("dve", t2[i][:], ps[pp][:, :], cstab[:, 1, cs], ALU.mult, [psB[pp], cstabB], [t2B[i]])
                    self.tt("dve", dst(c), t1[i][:], t2[i][:], ALU.add, [t1B[i], t2B[i]], [dB[c]])
            self.release(m_ab)
            if "idx" in self.debug:
                d = ddbg("qiT", [128, 2, T])
                self.dump_bf16(d, qiT[:], [128, 2, T], qiB)
                d = ddbg("kiT", [128, 1, T])
                self.dump_bf16(d, kiT[:].rearrange("p (o t) -> p o t", o=1), [128, 1, T], kiB)
                d = ddbg("vxe", [128, 32, 128])
                self.dump_bf16(d, vxe[:].rearrange("p a g d -> p (a g) d"), [128, 32, 128], vB)
                d = ddbg("wts", [128, 16, 8])
                fw.dma("sp", d, wts[:], reads=wB)

            if stop_after == "A":
                self.finalize_stub(outT)
                return nc


            fw.barrier()
            fw.dma("pool", wout_sb[:].rearrange("p k n -> p (k n)"), wout_d, writes=[B_wout])
            MNEG = -30000.0
            score = self.sb("score", [128, 4, T], F32)
            scB = [Buf("sc%d" % b) for b in range(4)]
            NRB = 3
            rbuf = [self.sb("rbuf%d" % i, [128, CH], F32) for i in range(NRB)]
            rbB = [Buf() for _ in range(NRB)]
            maskq = self.sb("maskq", [128, 4, T], BF16)
            mqB = [Buf("mq%d" % b) for b in range(4)]
            junk = maskq[:, 0, :]
            junkB = mqB[0]
            maskT = [self.sb("maskT%d" % i, [128, 16, CH], BF16) for i in range(2)]
            mTB = [[Buf("mT%d_%d" % (i, j)) for j in range(16)] for i in range(2)]
            NEB = 5
            ebuf = [self.sb("ebuf%d" % i, [128, CH], BF16) for i in range(NEB)]
            ebB = [Buf() for _ in range(NEB)]
            rden = [self.sb("rden%d" % i, [128, CH], F32) for i in range(2)]
            rdB = [Buf() for _ in range(2)]
            NH = NITER + 2
            Rv = self.sb("Rv", [128, 4], F32)
            Rp = self.sb("Rp", [128, 4], F32)
            Rn = self.sb("Rn", [128, 4], F32)
            RnB = Buf("Rn")
            halfs = self.sb("halfs", [128, 4, NH], F32)
            mid = self.sb("mid", [128, 4], F32)
            midp = self.sb("midp", [128, 4], F32)
            tsel = self.sb("tsel", [128, 4], F32)
            cnt = self.sb("cnt", [128, 4], F32)
            thr_all = self.sb("thr_all", [128, 16], F32)
            RvB, RpB, hfB, midB, midpB, tselB, cntB, thrB = (Buf("Rv"), Buf("Rp"), Buf("halfs"), Buf("mid"),
                                                            Buf("midp"), Buf("tsel"), Buf("cnt"), Buf("thr"))
            stB = {"ibank": 0, "ri": 0, "ei": 0, "sbank": 0}
            COL = {0: 0, 1: 1, 2: 2, 3: 3}
            junk2 = maskq[:, 1, :]
            junk2B = mqB[1]
            sacc = self.sb("sacc", [128, 4], F32)
            saccB = Buf("sacc")

            def units_I(c):
                us = []
                for b in range(4):
                    i = 4 * c + b
                    qs = slice(i * 128, (i + 1) * 128)
                    for kc2 in range(c + 1):
                        def u(b=b, i=i, qs=qs, kc2=kc2):
                            n = CH if kc2 < c else 128 * (b + 1)
                            ks = slice(kc2 * CH, kc2 * CH + n)
                            for h in range(8):
                                pb = 4 + stB["ibank"] % 4
                                stB["ibank"] += 1
                                r0 = 32 * (h % 4)
                                self.mm(ps[pb][:, 0:n], qiT[r0:r0 + 32, h // 4, qs], kiT[r0:r0 + 32, ks], True, True,
                                        [qiB[c], kiB[kc2]], [psB[pb]], tp=(r0, 0))
                                rb = stB["ri"] % NRB
                                stB["ri"] += 1
                                self.act(rbuf[rb][:, 0:n], ps[pb][:, 0:n], AF.Relu, [psB[pb]], [rbB[rb]])
                                if h == 0 and c == 3:
                                    self.act(score[:, b, ks], rbuf[rb][:, 0:n], AF.Identity, [rbB[rb], wB[c]], [scB[b]],
                                             scale=wts[:, i, 0:1])
                                elif h == 0:
                                    self.ts("dve", score[:, b, ks], rbuf[rb][:, 0:n], wts[:, i, 0:1], None, ALU.mult, None,
                                            [rbB[rb], wB[c]], [scB[b]])
                                else:
                                    self.stt(score[:, b, ks], rbuf[rb][:, 0:n], wts[:, i, h:h + 1], score[:, b, ks],
                                             ALU.mult, ALU.add, [rbB[rb], wB[c], scB[b]], [scB[b]])
                        us.append(u)

                    def u2(b=b, i=i):
                        ncol = 128 * (i + 1)
                        cb_ = COL[b]
                        self.ts("dve", junk[:, 0:ncol], score[:, b, 0:ncol], 1.0, -3.0e38, ALU.mult, ALU.max,
                                [scB[b]], [junkB, RvB], accum_out=Rv[:, cb_:cb_ + 1])
                        self.ts("dve", junk[:, 0:ncol], score[:, b, 0:ncol], -1.0, -3.0e38, ALU.mult, ALU.max,
                                [scB[b]], [junkB, RnB], accum_out=Rn[:, cb_:cb_ + 1])
                        self.tt("dve", score[:, b, i * 128:(i + 1) * 128], score[:, b, i * 128:(i + 1) * 128],
                                self.cst("cmask"), ALU.add, [scB[b], B_cst], [scB[b]])
                    us.append(u2)
                return us

            def units_S(c):
                us = []
                mb = c % 2

                def u0():
                    self.tt("dve", Rv[:, :], Rv[:, :], Rn[:, :], ALU.max, [RvB, RnB], [RvB])
                    self.ts("dve", Rp[:, :], Rv[:, :], 1.001, 1e-6, ALU.mult, ALU.add, [RvB], [RpB])
                    for b in range(4):
                        self.ts("dve", halfs[:, b, :], self.cst("pw2"), Rp[:, b:b + 1], None, ALU.mult, None,
                                [RpB, B_cst], [hfB])
                    self.memset("dve", mid[:, :], 0.0, [midB])
                us.append(u0)
                for k in range(1, NITER + 1):
                    def uk(k=k):
                        if k < NITER:
                            Hs, H2 = halfs[:, :, k], halfs[:, :, k - 1]
                        else:
                            Hs, H2 = halfs[:, :, k - 1], halfs[:, :, k - 1]
                        act_tiles = (2, 3)
                        dve_tiles = tuple(b for b in range(4) if b not in act_tiles)
                        for b in act_tiles:
                            ncol = 128 * (4 * c + b + 1)
                            cb_ = COL[b]
                            self.act(junk2[:, 0:ncol], score[:, b, 0:ncol], AF.Sign, [scB[b], midB], [junk2B, saccB],
                                     scale=-1.0, bias=mid[:, cb_:cb_ + 1], accum_out=sacc[:, cb_:cb_ + 1])
                        self.tt("dve", midp[:, :], mid[:, :], Hs, ALU.subtract, [midB, hfB], [midpB])
                        for b in dve_tiles:
                            ncol = 128 * (4 * c + b + 1)
                            cb_ = COL[b]
                            self.ts("dve", junk[:, 0:ncol], score[:, b, 0:ncol], mid[:, cb_:cb_ + 1], 0.0, ALU.is_ge, ALU.add,
                                    [scB[b], midB], [junkB, cntB], accum_out=cnt[:, cb_:cb_ + 1])
                        nd = len(dve_tiles)
                        self.stt(tsel[:, 0:nd], cnt[:, 0:nd], float(TOPK), H2[:, 0:nd], ALU.is_ge, ALU.mult, [cntB, hfB], [tselB])
                        for b in act_tiles:
                            ncol = 128 * (4 * c + b + 1)
                            cb_ = COL[b]
                            self.stt(tsel[:, cb_:cb_ + 1], sacc[:, cb_:cb_ + 1], float(ncol - 2 * TOPK), H2[:, cb_:cb_ + 1],
                                     ALU.is_le, ALU.mult, [saccB, hfB], [tselB])
                        self.tt("dve", mid[:, :], midp[:, :], tsel[:, :], ALU.add, [midpB, tselB], [midB])
                    us.append(uk)

                def um():
                    for b in range(4):
                        ncol = 128 * (4 * c + b + 1)
                        cb_ = COL[b]
                        if "bout" in self.debug:
                            self.copy("dve", thr_all[:, 4 * c + b:4 * c + b + 1], mid[:, cb_:cb_ + 1], [midB], [thrB])
                        self.ts("dve", maskq[:, b, 0:ncol], score[:, b, 0:ncol], mid[:, cb_:cb_ + 1], MNEG, ALU.is_lt, ALU.mult,
                                [scB[b], midB], [mqB[b]])
                us.append(um)
                for j in range(4 * c + 4):
                    def ut(j=j):
                        b0 = max(0, j - 4 * c)
                        tb = 0
                        for b in range(b0, 4):
                            self.tr(psT[tb][:, b * 128:(b + 1) * 128], maskq[:, b, j * 128:(j + 1) * 128], ident,
                                    [mqB[b], B_cstb], [psTB[tb]], signal=(b == 3))
                        self.copy("dve", maskT[mb][:, j, b0 * 128:CH], psT[tb][:, b0 * 128:CH], [psTB[tb]], [mTB[mb][j]])
                    us.append(ut)
                return us

            def units_A(c):
                us = []
                cs = slice(c * CH, (c + 1) * CH)
                mb = c % 2
                nj = 4 * c + 4
                norm_pending = []
                for h in range(8):
                    g, p, base = h // 4, h // 2, 64 * (h % 2)
                    vx = vxe if h % 2 == 0 else vxo
                    accb = 2 + (h % 2)
                    hstate = {"pend": []}
                    DEPTH = 2
                    for step in range(nj + DEPTH):
                        def ustep(step=step, h=h, g=g, p=p, base=base, vx=vx, accb=accb, hstate=hstate):
                            if step < nj:
                                j = step
                                sb_ = stB["sbank"] % 2
                                stB["sbank"] += 1
                                q0 = 128 * max(0, j - 4 * c)
                                qcs = slice(c * CH + q0, (c + 1) * CH)
                                self.mm(ps[sb_][:, q0:CH], kTz[:, g, h % 2, j * 128:(j + 1) * 128], qT[:, p, qcs],
                                        True, False, [kB[j // 4], qB[c]], [psB[sb_]], signal=False)
                                self.mm(ps[sb_][:, q0:CH], ident, maskT[mb][:, j, q0:CH], False, True, [B_cstb, mTB[mb][j]],
                                        [psB[sb_]])
                                e = stB["ei"] % NEB
                                stB["ei"] += 1
                                self.act(ebuf[e][:, q0:CH], ps[sb_][:, q0:CH], AF.Exp, [psB[sb_]], [ebB[e]], scale=0.125)
                                hstate["pend"].append((j, e, q0))
                            if step >= DEPTH:
                                pj, pe_, pq0 = hstate["pend"].pop(0)
                                self.mm(ps[accb][:, pq0:CH], vx[:, pj, g, :], ebuf[pe_][:, pq0:CH], pj == 0, pj == nj - 1,
                                        [vB[pj // 4], ebB[pe_]], [psB[accb]], signal=True)
                        us.append(ustep)

                    def unorm_act(h=h, p=p, accb=accb):
                        rd = h % 2
                        rows = slice(64, 128) if h % 2 == 0 else slice(0, 64)
                        self.act(rden[rd][rows, :], ps[accb][rows, :], AF.Ln, [psB[accb]], [rdB[rd]])
                        self.act(rden[rd][rows, :], rden[rd][rows, :], AF.Exp, [rdB[rd]], [rdB[rd]], scale=-1.0)

                    def unorm_dve(h=h, p=p, accb=accb):
                        rd = h % 2
                        if h % 2 == 0:
                            self.tt("dve", mixT[0:64, 4 + p, cs], ps[accb][0:64, :], rden[rd][64:128, :], ALU.mult,
                                    [psB[accb], rdB[rd]], [mixB[4 + p][c]])
                        else:
                            self.tt("dve", mixT[64:128, 4 + p, cs], ps[accb][64:128, :], rden[rd][0:64, :], ALU.mult,
                                    [psB[accb], rdB[rd]], [mixB[4 + p][c]])
                    us.append(unorm_act)
                    if norm_pending:
                        us.append(norm_pending.pop(0))
                    norm_pending.append(unorm_dve)
                us.extend(norm_pending)
                return us

            def interleave(la, lb):
                na, nb = len(la), len(lb)
                ia = ib = 0
                while ia < na or ib < nb:
                    if ib >= nb or (ia < na and ia * nb <= ib * na):
                        la[ia]()
                        ia += 1
                    else:
                        lb[ib]()
                        ib += 1

            order = [3, 2, 1, 0]
            for u in units_I(order[0]) + units_S(order[0]):
                u()
            for oi, c in enumerate(order):
                la = units_A(c)
                lb = (units_I(order[oi + 1]) + units_S(order[oi + 1])) if oi + 1 < NCH else []
                interleave(la, lb)
            if "bout" in self.debug:
                d = ddbg("thr", [128, 16])
                fw.dma("sp", d, thr_all[:], reads=[thrB])
                d = ddbg("bout", [128, 4, T])
                self.dump_bf16(d, mixT[:, 4:8, :], [128, 4, T], [mixB[4 + p][c] for p in range(4) for c in range(NCH)])
            fw.barrier()
            self.release(m_persist)
            if stop_after == "B":
                self.finalize_stub(outT)
                return nc

            x1T = self.sbtop("x1T", [128, KC, T], F32)
            x1B = [[Buf("x1_%d_%d" % (n, c)) for c in range(NCH)] for n in range(KC)]
            HT = 1024
            hnT = self.sbtop("hnT", [128, KC, HT], BF16)
            hnB = [Buf("hn0"), Buf("hn1")]

            def norm2_cc(c, cc, pb, sq3, sq3Bs, sr3t, sr3Bs):
                cs = slice(c * CH, (c + 1) * CH)
                hs = slice(cc * CH, (cc + 1) * CH)
                self.act(sq3[:], x1T[:, :, cs], AF.Square, [x1B[n][c] for n in range(KC)], sq3Bs)
                for kc in range(KC):
                    self.mm(ps[pb][:, :], ones, sq3[:, kc, :], kc == 0, kc == KC - 1, sq3Bs + [B_cstb], [psB[pb]],
                            signal=(kc == KC - 1))
                self.act(sr3t[:, 0, :], ps[pb][:, :], AF.Ln, [psB[pb], B_cst], sr3Bs, scale=1.0 / D, bias=eps_ap)
                self.act(sr3t[:, 1, :], sr3t[:, 0, :], AF.Exp, sr3Bs, sr3Bs, scale=-0.5)
                for kc in range(KC):
                    self.stt(hnT[:, kc, hs], x1T[:, kc, cs], self.cst("g2", kc, 1), sr3t[:, 1, :], ALU.mult, ALU.mult,
                             [x1B[kc][c], B_cst] + sr3Bs, [hnB[cc]])
            NWU = 3
            wu = [self.sbtop("wu%d" % i, [128, 2, KC, 128], BF16) for i in range(NWU)]
            wuB = [Buf() for _ in range(NWU)]
            sq3c = self.sb("sq3c", [128, KC, CH], BF16)
            sr3c = self.sb("sr3c", [128, 2, CH], F32)
            sq3cB, sr3cB = [Buf("sq3c")], [Buf("sr3c")]
            NXR = 2
            xres = [self.sb("xres%d" % i, [128, 2, KC, 256], F32) for i in range(NXR)]
            xrB = [[Buf(), Buf()] for _ in range(NXR)]

            def xr_dma(c):
                for a in range(2):
                    fw.dma("sp" if a == 0 else "act", xres[c % NXR][:, a], xT[:, 2 * c + a], writes=[xrB[c % NXR][a]])
            for c in range(NXR):
                xr_dma(c)
            cnt_c = 0
            for c in range(NCH):
                cs = slice(c * CH, (c + 1) * CH)
                xi = c % NXR
                for n in range(KC):
                    if c in (1, 2) and n == 2:
                        norm2_cc(c - 1, c - 1, cnt_c % 8, sq3c, sq3cB, sr3c, sr3cB)
                        cnt_c += 1
                    pb = cnt_c % 8
                    cnt_c += 1
                    for ct in range(8):
                        self.mm(ps[pb][:, :], wout_sb[:, ct, n * 128:(n + 1) * 128], mixT[:, ct, cs], ct == 0, ct == 7,
                                [B_wout, mixB[ct][c]], [psB[pb]], signal=(ct == 7))
                    self.tt("dve", x1T[:, n, cs].rearrange("p (a t) -> p a t", a=2),
                            ps[pb][:, :].rearrange("p (a t) -> p a t", a=2), xres[xi][:, :, n, :], ALU.add,
                            [psB[pb], xrB[xi][0], xrB[xi][1]], [x1B[n][c]])
                if c + NXR < NCH:
                    xr_dma(c + NXR)
                if c == 1:
                    for i in range(2):
                        fw.dma("pool", wu[i][:].rearrange("p g k n -> p (g k n)"), wup_d[i], writes=[wuB[i]])
            if "x1" in self.debug:
                d = ddbg("x1", [128, KC, T])
                for n in range(KC):
                    fw.dma("sp", d[:, n, :], x1T[:, n, :], reads=x1B[n])
            self.release(m_const)
            if stop_after == "C":
                self.finalize_stub(outT)
                return nc

            actT = self.sb("actT", [128, NFT, HT], BF16)
            actB = [Buf("act%d" % i) for i in range(NFT)]
            NUB = 3
            Ug = [self.sb("Ug%d" % i, [128, 2 + HT], F32) for i in range(NUB)]
            Uv = [self.sb("Uv%d" % i, [128, 2 + HT], F32) for i in range(NUB)]
            UgB = [Buf() for _ in range(NUB)]
            UvB = [Buf() for _ in range(NUB)]
            NCB = 3
            cgv = []
            offs_cgv = []
            for i in range(NCB):
                offs_cgv.append(self.sb_off)
                cgv.append(self.sb("cgv%d" % i, [128, 2, HT], F32))
            cg = [cgv[i][:, 0, :] for i in range(NCB)]
            cv = [cgv[i][:, 1, :] for i in range(NCB)]
            cgB = [Buf() for _ in range(NCB)]
            cvB = [Buf() for _ in range(NCB)]
            self.n_t += 1
            sq3 = nc.alloc_sbuf_tensor_at("sq3_%d" % self.n_t, [128, KC, CH], BF16, offset=offs_cgv[NCB - 1])
            self.n_t += 1
            sr3t = nc.alloc_sbuf_tensor_at("sr3_%d" % self.n_t, [128, 2, CH], F32, offset=offs_cgv[NCB - 2])
            sq3Bs = [cgB[NCB - 1], cvB[NCB - 1]]
            sr3Bs = [cgB[NCB - 2]]
            halo = self.sb("halo", [128, 2 * NFT, 2], F32)
            haloB = [Buf() for _ in range(2 * NFT)]
            NWD = 2
            wd = [self.sb("wd%d" % i, [128, NFT, 128], BF16) for i in range(NWD)]
            wdB = [Buf() for _ in range(NWD)]
            ost = [self.sb("ost%d" % i, [128, CH], F32) for i in range(2)]
            ostB = [Buf() for _ in range(2)]
            out_toks = []
            st_ = {"bank": 0, "osi": 0}
            fw.max_pool_outstanding = 6

            def load_wu(hf, i):
                w_ = (hf * NFT + i) % NWU
                fw.dma("pool", wu[w_][:].rearrange("p g k n -> p (g k n)"), wup_d[i], writes=[wuB[w_]])

            def load_wd(hf, n):
                d_ = (hf * KC + n) % NWD
                fw.dma("pool", wd[d_][:].rearrange("p f m -> p (f m)"), wdn_d[n], writes=[wdB[d_]])

            def stage0(hf, i):
                w_ = (hf * NFT + i) % NWU
                ub = (hf * NFT + i) % NUB
                cb = (hf * NFT + i) % NCB
                for (gi, U, UB, fidx) in ((0, Ug[ub], UgB[ub], i), (1, Uv[ub], UvB[ub], NFT + i)):
                    for cc in range(2):
                        pb = st_["bank"] % 8
                        st_["bank"] += 1
                        for kc in range(KC):
                            self.mm(ps[pb][:, :], wu[w_][:, gi, kc, :], hnT[:, kc, cc * CH:(cc + 1) * CH],
                                    kc == 0, kc == KC - 1, [wuB[w_], hnB[cc]], [psB[pb]], signal=(kc == KC - 1))
                        self.copy("act", U[:, 2 + cc * CH:2 + (cc + 1) * CH], ps[pb][:, :], [psB[pb]], [UB])
                        c0, c0B = (cg[cb], cgB[cb]) if gi == 0 else (cv[cb], cvB[cb])
                        self.act(c0[:, cc * CH:(cc + 1) * CH], ps[pb][:, :], AF.Identity, [psB[pb], B_cst], [c0B],
                                 scale=self.cst("cw", 2 * 2 * NFT + fidx, 1), bias=self.cst("cb", fidx, 1))

            def halo_in(hf, i):
                ub = (hf * NFT + i) % NUB
                for (U, UB, fidx) in ((Ug[ub], UgB[ub], i), (Uv[ub], UvB[ub], NFT + i)):
                    if hf == 0:
                        if i < NUB:
                            self.memset("dve", U[:, 0:2], 0.0, [UB])
                    else:
                        self.copy("dve", U[:, 0:2], halo[:, fidx, :], [haloB[fidx]], [UB])

            def stage1(hf, i):
                ub = (hf * NFT + i) % NUB
                cb = (hf * NFT + i) % NCB
                for (U, UB, fidx, cc_, ccB) in ((Ug[ub], UgB[ub], i, cg[cb], cgB[cb]),
                                                (Uv[ub], UvB[ub], NFT + i, cv[cb], cvB[cb])):
                    if hf == 0:
                        self.copy("dve", halo[:, fidx, :], U[:, HT:HT + 2], [UB], [haloB[fidx]])
                    w0 = self.cst("cw", 0 * 2 * NFT + fidx, 1)
                    w1 = self.cst("cw", 1 * 2 * NFT + fidx, 1)
                    self.stt(cc_, U[:, 1:1 + HT], w1, cc_, ALU.mult, ALU.add, [UB, B_cst, ccB], [ccB])
                    self.stt(cc_, U[:, 0:HT], w0, cc_, ALU.mult, ALU.add, [UB, B_cst, ccB], [ccB])

            def stage2(hf, i):
                cb = (hf * NFT + i) % NCB
                self.act(cg[cb], cg[cb], AF.Silu, [cgB[cb]], [cgB[cb]])
                self.tt("dve", actT[:, i, :], cg[cb], cv[cb], ALU.mult, [cgB[cb], cvB[cb]], [actB[i]])

            def norm2(hf):
                for cc in range(2):
                    pb = st_["bank"] % 8
                    st_["bank"] += 1
                    norm2_cc(2 * hf + cc, cc, pb, sq3, sq3Bs, sr3t, sr3Bs)

            for hf in range(2):
                for it in range(NFT + 2):
                    if it < NFT:
                        halo_in(hf, it)
                    if it >= 2:
                        stage2(hf, it - 2)
                    if it < NFT:
                        if it + 2 < NFT:
                            load_wu(hf, it + 2)
                        stage0(hf, it)
                    if 1 <= it <= NFT:
                        stage1(hf, it - 1)
                    if it == NFT - 1:
                        load_wd(hf, 0)
                for n in range(KC):
                    d_ = (hf * KC + n) % NWD
                    if n + 1 < KC:
                        load_wd(hf, n + 1)
                    if hf == 0 and n == 1:
                        norm2(1)
                        load_wu(1, 0)
                        load_wu(1, 1)
                    for cc in range(2):
                        c = 2 * hf + cc
                        cs = slice(c * CH, (c + 1) * CH)
                        pb = st_["bank"] % 8
                        st_["bank"] += 1
                        for ft in range(NFT):
                            self.mm(ps[pb][:, :], wd[d_][:, ft, :], actT[:, ft, cc * CH:(cc + 1) * CH], ft == 0, ft == NFT - 1,
                                    [wdB[d_], actB[ft]], [psB[pb]], signal=(ft == NFT - 1))
                        o_ = st_["osi"] % 2
                        st_["osi"] += 1
                        self.tt("dve", ost[o_][:], ps[pb][:, :], x1T[:, n, cs], ALU.add, [psB[pb], x1B[n][c]], [ostB[o_]])
                        out_toks.append(fw.dma("sp", outT[:, n, cs], ost[o_][:], reads=[ostB[o_]]))
            fw.barrier()
            fw.finish()
            return nc

    def dump_bf16(self, d, src_ap, shape, bufs):
        fw = self.fw
        m = self.mark()
        n1 = shape[1]
        stg = self.sb("dbgstg", [128, shape[2]], F32)
        sB = Buf("dbgstg")
        for a in range(n1):
            self.copy("dve", stg[:, :], src_ap[:, a, :], list(bufs), [sB])
            fw.dma("sp", d[:, a, :], stg[:, :], reads=[sB])
        fw.barrier()
        self.release(m)

    def finalize_stub(self, outT):
        fw = self.fw
        m = self.mark()
        z = self.sb("zstub", [128, 512], F32)
        zB = Buf()
        self.memset("dve", z[:], 0.0, [zB])
        for kc in range(KC):
            for c in range(NCH):
                fw.dma("sp", outT[:, kc, c * CH:(c + 1) * CH], z[:], reads=[zB])
        fw.barrier()
        fw.finish()


def _rope_tables(head_dim, reps):
    half = head_dim // 2
    inv = np.exp(-np.log(np.float32(10000.0)) * np.arange(half, dtype=np.float32) / half).astype(np.float32)
    pos = np.arange(T, dtype=np.float32)
    ang = (pos[:, None] * inv[None, :]).astype(np.float32)
    cos = np.cos(ang).astype(np.float32).T
    sin = np.sin(ang).astype(np.float32).T
    cos_full = np.concatenate([cos, cos], 0)
    sin_signed = np.concatenate([-sin, sin], 0)
    tab = np.stack([np.tile(cos_full, (reps, 1)), np.tile(sin_signed, (reps, 1))], 1)
    return np.ascontiguousarray(tab.astype(np.float32))


def _perm(head_dim):
    half = head_dim // 2
    return np.concatenate([np.arange(half, head_dim), np.arange(0, half)])


def _tile_k(w):
    n = w.shape[1]
    return np.ascontiguousarray(w.reshape(KC, 128, n).transpose(1, 0, 2).reshape(128, KC * n))


def host_prepare(inp):
    f = np.float32
    w_in = np.asarray(inp["w_in"][0], f)
    cuts = np.cumsum([512, 512, 128, 128, 256, 32, 8])
    w_vp = w_in[:, 0:cuts[0]]
    w_q = w_in[:, cuts[0]:cuts[1]]
    w_k = w_in[:, cuts[1]:cuts[2]]
    w_v = w_in[:, cuts[2]:cuts[3]]
    w_qi = w_in[:, cuts[3]:cuts[4]]
    w_ki = w_in[:, cuts[4]:cuts[5]]
    w_wi = w_in[:, cuts[5]:cuts[6]]
    p64 = _perm(64)
    p32 = _perm(32)
    tiles = []
    for g in range(4):
        tiles.append(w_vp[:, g * 128:(g + 1) * 128])
    wq_h = w_q.reshape(D, 8, 64)
    for p in range(4):
        tiles.append(wq_h[:, 2 * p:2 * p + 2, :].reshape(D, 128))
    for p in range(4):
        tiles.append(wq_h[:, 2 * p:2 * p + 2, :][:, :, p64].reshape(D, 128))
    wk_h = w_k.reshape(D, 2, 64)
    for g in range(2):
        tiles.append(np.concatenate([wk_h[:, g, :], wk_h[:, g, :]], 1))
    for g in range(2):
        tiles.append(np.concatenate([wk_h[:, g, :][:, p64], wk_h[:, g, :][:, p64]], 1))
    wqi_h = w_qi.reshape(D, 8, 32)
    for a in range(2):
        tiles.append(wqi_h[:, 4 * a:4 * a + 4, :].reshape(D, 128))
    for a in range(2):
        tiles.append(wqi_h[:, 4 * a:4 * a + 4, :][:, :, p32].reshape(D, 128))
    tiles.append(np.tile(w_ki, (1, 4)))
    tiles.append(np.tile(w_ki[:, p32], (1, 4)))
    assert len(tiles) == NWIN
    win = np.stack([_tile_k(t) for t in tiles], 0)
    wvw = _tile_k(np.concatenate([w_v, w_wi], 1))
    poolw = np.ascontiguousarray(np.asarray(inp["pool_w"][0], f).transpose(1, 0, 2).reshape(128, 4 * 128))
    wout = _tile_k(np.asarray(inp["w_out"][0], f))
    w_up = np.asarray(inp["w_up"][0], f)
    wup = np.stack([np.concatenate([_tile_k(w_up[:, t * 128:(t + 1) * 128]),
                                    _tile_k(w_up[:, (NFT + t) * 128:(NFT + t + 1) * 128])], 1) for t in range(NFT)], 0)
    w_dn = np.asarray(inp["w_down"][0], f)
    wdn = np.stack([np.ascontiguousarray(w_dn[:, n * 128:(n + 1) * 128].reshape(NFT, 128, 128)
                                         .transpose(1, 0, 2).reshape(128, NFT * 128)) for n in range(8)], 0)
    cst = np.zeros((128, NCST), f)

    def put(name, arr):
        o, n = CST[name]
        cst[:, o:o + n] = np.asarray(arr, f).reshape(128, n)
    put("g1", np.asarray(inp["norm1_g"][0], f).reshape(KC, 128).T)
    put("g2", np.asarray(inp["norm2_g"][0], f).reshape(KC, 128).T)
    gq = np.asarray(inp["q_norm_g"][0], f)
    gk = np.asarray(inp["k_norm_g"][0], f)
    put("gq", np.tile(gq, 2))
    put("gqp", np.tile(gq[p64], 2))
    put("gk", np.tile(gk, 2))
    put("gkp", np.tile(gk[p64], 2))
    put("ps", np.asarray(inp["pool_scale"][0], f).reshape(4, 128).T)
    cw = np.asarray(inp["conv_w"][0], f).reshape(3, 2 * NFT, 128).transpose(2, 0, 1)
    put("cw", cw.reshape(128, 132))
    put("cb", np.asarray(inp["conv_b"][0], f).reshape(2 * NFT, 128).T)
    invc = np.zeros((4, 16), f)
    for g, w in enumerate((2, 4, 8, 16)):
        for t in range(16):
            invc[g, t] = 1.0 / min(t + 1, w)
    put("invc", np.tile(invc.reshape(1, 64), (128, 1)))
    pw2 = np.array([2.0 ** (-k) for k in range(NITER + 2)], f)
    put("pw2", np.tile(pw2[None, :], (128, 1)))
    tq = np.arange(128)[:, None]
    sk = np.arange(128)[None, :]
    put("cmask", np.where(sk <= tq, 0.0, NEG).astype(f))
    sel = np.zeros((128, 64), f)
    sel[64 + np.arange(64), np.arange(64)] = 1.0
    sel[np.arange(64), np.arange(64)] = 1.0
    put("sel", sel)
    put("eps", np.full((128, 1), EPS, f))
    cstb = np.zeros((128, 384), f)
    cstb[:, 0:128] = np.eye(128, dtype=f)
    cstb[:, 128:256] = 1.0
    cstb[0:64, 256:320] = 1.0
    cstb[64:128, 320:384] = 1.0
    shared = dict(cst=cst, cstb=cstb, csq=_rope_tables(64, 2), csi=_rope_tables(32, 4), win=win, wvw=wvw,
                  poolw=poolw, wout=wout, wup=wup, wdn=wdn)
    x = np.asarray(inp["x"], f)
    in_maps = []
    for b in range(x.shape[0]):
        xTb = np.ascontiguousarray(x[b].T.reshape(KC, 128, T // 256, 256).transpose(1, 2, 0, 3))
        m = dict(shared)
        m["xT"] = xTb
        in_maps.append(m)
    return in_maps


_NC_CACHE = {}


def kernel(**inputs):
    in_maps = host_prepare(inputs)
    if "nc" not in _NC_CACHE:
        _NC_CACHE["nc"] = K().build()
    nc = _NC_CACHE["nc"]
    res = run_bass_kernel_spmd(nc, in_maps, core_ids=list(range(8)))
    outs = []
    for r in res.results:
        oT = np.asarray(r["outT"], np.float32)
        outs.append(oT.transpose(1, 0, 2).reshape(D, T).T)
    return np.ascontiguousarray(np.stack(outs, 0).astype(np.float32))
```

```python
import math
from contextlib import ExitStack

import numpy as np
import concourse.bass as bass
import concourse.mybir as mybir
from concourse.bass_utils import run_bass_kernel_spmd

F32 = mybir.dt.float32
BF16 = mybir.dt.bfloat16
U32 = mybir.dt.uint32
ALU = mybir.AluOpType
AF = mybir.ActivationFunctionType
AX = mybir.AxisListType

T = 2048
D = 1024
NCH = 4
CH = 512
KC = 8
DFF = 2816
NFT = 22
EPS = 1e-6
TOPK = 256
NITER = 16
NEG = -1e30
SB_BASE = 16512
SB_END = 229344


class Buf:
    __slots__ = ("name", "w", "r", "excl")

    def __init__(self, name="", excl=False):
        self.name = name
        self.w = None
        self.r = []
        self.excl = excl


class Eng:
    def __init__(self, name, idx):
        self.name = name
        self.idx = idx
        self.sem = None
        self.count = 0
        self.known = {}
        self.prog = []


class FW:
    ENG_NAMES = ["pe", "act", "dve", "pool", "sp"]

    def __init__(self, nc, n_dma_sems=16):
        self.nc = nc
        self.E = {n: Eng(n, i) for i, n in enumerate(self.ENG_NAMES)}
        self.by_idx = {e.idx: e for e in self.E.values()}
        self.qsems = {"sp": list(range(0, 12)), "pool": list(range(12, 20)), "act": list(range(20, 24))}
        n_dma_sems = 24
        self.n_dma_sems = n_dma_sems
        self.dma_n = {"sp": 0, "pool": 0, "act": 0}
        self.dma_last_val = [0] * n_dma_sems
        self.n_ins = 0
        self.pool_dmas = []
        self.max_pool_outstanding = 2

    def setup(self, stack):
        nc = self.nc
        for n, e in self.E.items():
            e.sem = stack.enter_context(nc.semaphore("s_" + n))
        self.dma_sems = [stack.enter_context(nc.semaphore("s_dma%d" % i))
                         for i in range(self.n_dma_sems)]

    def _need_wait(self, E, tok, raw):
        kind = tok[0]
        if kind == "e":
            _, idx, c = tok
            if idx == E.idx:
                if E.name == "pe":
                    return None
            key = ("e", idx)
            if E.known.get(key, 0) >= c:
                return None
            E.known[key] = c
            return (self.by_idx[idx].sem, c)
        _, si, v = tok
        key = ("d", si)
        if E.known.get(key, 0) >= v:
            return None
        E.known[key] = v
        return (self.dma_sems[si], v)

    def _waits(self, E, reads, writes, extra):
        deps = []
        for b in reads:
            if b.w is not None:
                deps.append((b.w, True))
        for b in writes:
            if b.w is not None:
                deps.append((b.w, False))
            for t in b.r:
                deps.append((t, False))
        for t in extra:
            deps.append((t, True))
        waits = []
        for (d, raw) in deps:
            w = self._need_wait(E, d, raw)
            if w is not None:
                waits.append(w)
        return waits

    def _record(self, tok, reads, writes):
        for b in reads:
            b.r = [t for t in b.r if not (t[0] == tok[0] and t[1] == tok[1])]
            b.r.append(tok)
        for b in writes:
            b.w = tok
            b.r = []

    def op(self, eng, build, reads=(), writes=(), signal=True, extra=()):
        E = self.E[eng]
        if any(b.excl for b in reads):
            writes = list(writes) + [b for b in reads if b.excl]
            reads = [b for b in reads if not b.excl]
        waits = self._waits(E, reads, writes, extra)
        self.n_ins += 1
        if signal:
            E.count += 1
            tok = ("e", E.idx, E.count)
            sem = E.sem

            def f(h, waits=waits, build=build, sem=sem):
                for (s, v) in waits:
                    h.wait_ge(s, v)
                build(h).then_inc(sem, 1)
        else:
            tok = ("e", E.idx, E.count + 1)

            def f(h, waits=waits, build=build):
                for (s, v) in waits:
                    h.wait_ge(s, v)
                build(h)
        E.prog.append(f)
        self._record(tok, reads, writes)
        return tok

    def dma(self, queue, out_ap, in_ap, reads=(), writes=(), extra=()):
        E = self.E[queue]
        qs = self.qsems[queue]
        si = qs[self.dma_n[queue] % len(qs)]
        self.dma_n[queue] += 1
        prev = self.dma_last_val[si]
        val = prev + 16
        self.dma_last_val[si] = val
        tok = ("d", si, val)
        ex = list(extra)
        if prev > 0:
            ex.append(("d", si, prev))
        if queue == "pool":
            if len(self.pool_dmas) >= self.max_pool_outstanding:
                ex.append(self.pool_dmas[-self.max_pool_outstanding])
            self.pool_dmas.append(tok)
        waits = self._waits(E, reads, writes, ex)
        sem = self.dma_sems[si]

        def f(h, waits=waits, sem=sem, out_ap=out_ap, in_ap=in_ap):
            for (s, v) in waits:
                h.wait_ge(s, v)
            h.dma_start(out=out_ap, in_=in_ap).then_inc(sem, 16)
        E.prog.append(f)
        self._record(tok, reads, writes)
        return tok

    def all_tokens(self):
        toks = []
        for e in self.E.values():
            if e.count > 0:
                toks.append(("e", e.idx, e.count))
        for si, v in enumerate(self.dma_last_val):
            if v > 0:
                toks.append(("d", si, v))
        return toks

    def wait_all(self, eng, toks):
        E = self.E[eng]
        waits = []
        for d in toks:
            w = self._need_wait(E, d, True)
            if w is not None:
                waits.append(w)
        if not waits:
            return

        def f(h, waits=waits):
            for (s, v) in waits:
                h.wait_ge(s, v)
        E.prog.append(f)

    def barrier(self):
        toks = self.all_tokens()
        for n in self.ENG_NAMES:
            self.wait_all(n, toks)

    def finish(self):
        nc = self.nc
        fw = self
        with nc.Block() as block:
            @block.tensor
            def _(h):
                for f in fw.E["pe"].prog:
                    f(h)

            @block.scalar
            def _(h):
                for f in fw.E["act"].prog:
                    f(h)

            @block.vector
            def _(h):
                for f in fw.E["dve"].prog:
                    f(h)

            @block.gpsimd
            def _(h):
                for f in fw.E["pool"].prog:
                    f(h)

            @block.sync
            def _(h):
                for f in fw.E["sp"].prog:
                    f(h)


def _cst_layout():
    lay = {}
    off = 0
    for name, n in [("g1", 8), ("g2", 8), ("gq", 1), ("gqp", 1), ("gk", 1), ("gkp", 1),
                    ("ps", 4), ("cw", 132), ("cb", 44), ("invc", 64), ("pw2", NITER + 2),
                    ("cmask", 128), ("sel", 64), ("eps", 1), ("zero", 1)]:
        lay[name] = (off, n)
        off += n
    return lay, off


CST, NCST = _cst_layout()
NWIN = 22


class K:
    def __init__(self, debug=None):
        self.debug = debug or []
        self.nc = bass.Bass("TRN2", target_bir_lowering=False)
        self.fw = FW(self.nc)
        self.sb_off = SB_BASE
        self.top_off = SB_END
        self.n_t = 0

    def sb(self, name, shape, dt):
        esz = 4 if dt in (F32, U32) else 2
        sz = int(np.prod(shape[1:])) * esz
        sz = (sz + 63) // 64 * 64
        assert self.sb_off + sz <= self.top_off, (name, self.sb_off, sz)
        self.n_t += 1
        t = self.nc.alloc_sbuf_tensor_at("%s_%d" % (name, self.n_t), list(shape), dt, offset=self.sb_off)
        self.sb_off += sz
        return t

    def sbtop(self, name, shape, dt):
        esz = 4 if dt in (F32, U32) else 2
        sz = int(np.prod(shape[1:])) * esz
        sz = (sz + 63) // 64 * 64
        self.top_off -= sz
        self.top_off = self.top_off // 64 * 64
        assert self.top_off >= self.sb_off, (name, self.top_off, self.sb_off)
        self.n_t += 1
        return self.nc.alloc_sbuf_tensor_at("%s_%d" % (name, self.n_t), list(shape), dt, offset=self.top_off)

    def mark(self):
        return self.sb_off

    def release(self, m):
        self.fw.barrier()
        self.sb_off = m

    def mm(self, out, lhsT, rhs, start, stop, reads, writes, signal=True, tp=None):
        if tp is None:
            return self.fw.op("pe", lambda h: h.matmul(out, lhsT, rhs, start=start, stop=stop),
                              reads, writes, signal)
        return self.fw.op("pe", lambda h: h.matmul(out, lhsT, rhs, start=start, stop=stop, tile_position=tp),
                          reads, writes, signal)

    def tr(self, out, in_, ident, reads, writes, signal=True):
        return self.fw.op("pe", lambda h: h.transpose(out, in_, ident), reads, writes, signal)

    def act(self, out, in_, func, reads, writes, scale=1.0, bias=None, accum_out=None):
        kw = {}
        if bias is not None:
            kw["bias"] = bias
        if accum_out is not None:
            kw["accum_out"] = accum_out
        return self.fw.op("act", lambda h: h.activation(out, in_, func, scale=scale, **kw), reads, writes)

    def tt(self, eng, out, in0, in1, op, reads, writes):
        return self.fw.op(eng, lambda h: h.tensor_tensor(out, in0, in1, op), reads, writes)

    def ts(self, eng, out, in0, s1, s2, op0, op1, reads, writes, accum_out=None):
        if op1 is None:
            return self.fw.op(eng, lambda h: h.tensor_scalar(out, in0, s1, None, op0), reads, writes)
        if accum_out is None:
            return self.fw.op(eng, lambda h: h.tensor_scalar(out, in0, s1, s2, op0, op1), reads, writes)
        return self.fw.op(eng, lambda h: h.tensor_scalar(out, in0, s1, s2, op0, op1, accum_out=accum_out),
                          reads, writes)

    def stt(self, out, in0, scalar, in1, op0, op1, reads, writes):
        return self.fw.op("dve", lambda h: h.scalar_tensor_tensor(out, in0, scalar, in1, op0, op1), reads, writes)

    def copy(self, eng, out, in_, reads, writes):
        if eng == "act":
            return self.fw.op("act", lambda h: h.activation(out, in_, AF.Copy), reads, writes)
        return self.fw.op(eng, lambda h: h.tensor_copy(out, in_), reads, writes)

    def recip(self, out, in_, reads, writes):
        return self.fw.op("dve", lambda h: h.reciprocal(out, in_), reads, writes)

    def memset(self, eng, ap, val, writes):
        return self.fw.op(eng, lambda h: h.memset(ap, val), (), writes)

    def cst(self, name, a=0, n=None):
        o, w = CST[name]
        if n is None:
            n = w - a
        return self.cst_t[:, o + a:o + a + n]

    def build(self, stop_after=None):
        nc = self.nc
        fw = self.fw
        dram = {}

        def din(name, shape):
            dram[name] = nc.dram_tensor(name, list(shape), F32, kind="ExternalInput").ap()
            return dram[name]

        xT = din("xT", [128, T // 256, KC, 256])
        cst_d = din("cst", [128, NCST])
        cstb_d = din("cstb", [128, 384])
        csq_d = din("csq", [128, 2, T])
        csi_d = din("csi", [128, 2, T])
        win_d = din("win", [NWIN, 128, KC * 128])
        wvw_d = din("wvw", [128, KC * 136])
        poolw_d = din("poolw", [128, 4 * 128])
        wout_d = din("wout", [128, 8 * D])
        wup_d = din("wup", [NFT, 128, 2 * KC * 128])
        wdn_d = din("wdn", [8, 128, NFT * 128])
        outT = nc.dram_tensor("outT", [128, KC, T], F32, kind="ExternalOutput").ap()
        dbg = {}
        self.dbg = dbg

        def ddbg(name, shape):
            dbg[name] = nc.dram_tensor("dbg_" + name, list(shape), F32, kind="ExternalOutput").ap()
            return dbg[name]

        with ExitStack() as st:
            fw.setup(st)
            ps = [st.enter_context(nc.psum_tensor("ps%d" % i, [128, 512], F32)) for i in range(8)]
            psB = [Buf("ps%d" % i, excl=True) for i in range(8)]
            psT = [ps[4][:, :].bitcast(BF16)]
            psTB = [psB[4]]
            self.ps, self.psB, self.psT, self.psTB = ps, psB, psT, psTB

            self.cst_t = self.sb("cst", [128, NCST], F32)
            cstb = self.sb("cstb", [128, 384], BF16)
            B_cst = Buf("cst")
            B_cstb = Buf("cstb")
            fw.dma("sp", self.cst_t[:], cst_d, writes=[B_cst])
            fw.dma("pool", cstb[:], cstb_d, writes=[B_cstb])
            ident = cstb[:, 0:128]
            ones = cstb[:, 128:256]
            blockones = cstb[:, 256:384]
            eps_ap = self.cst("eps")
            self.B_cst, self.B_cstb = B_cst, B_cstb

            m_const = self.mark()
            off_mixT = self.sb_off
            mixT = self.sb("mixT", [128, 8, T], BF16)
            mixB = [[Buf("mix%d_%d" % (s, c)) for c in range(NCH)] for s in range(8)]
            off_wout = self.sb_off
            wout_sb = self.sb("wout", [128, 8, D], BF16)
            B_wout = Buf("wout")
            m_persist = self.mark()

            qT = self.sb("qT", [128, 4, T], BF16)
            kTz = self.sb("kTz", [128, 2, 2, T], BF16)
            vxe = self.sb("vxe", [128, 16, 2, 128], BF16)
            vxo = self.sb("vxo", [128, 16, 2, 128], BF16)
            qiT = self.sb("qiT", [128, 2, T], BF16)
            kiT = self.sb("kiT", [128, T], BF16)
            wts = self.sb("wts", [128, 16, 8], F32)
            qB = [Buf("q%d" % c) for c in range(NCH)]
            kB = [Buf("k%d" % c) for c in range(NCH)]
            vB = [Buf("v%d" % c) for c in range(NCH)]
            qiB = [Buf("qi%d" % c) for c in range(NCH)]
            kiB = [Buf("ki%d" % c) for c in range(NCH)]
            wB = [Buf("w%d" % c) for c in range(NCH)]
            m_ab = self.mark()

            xnT = self.sb("xnT", [128, KC, T], BF16)
            xnB = [Buf("xn%d" % c) for c in range(NCH)]
            m_a = self.mark()
            wpl = self.sb("wpl", [128, 4, KC, 128], BF16)
            wplB = [Buf("wpl%d" % g) for g in range(4)]
            poolw = self.sb("poolw", [128, 4, 128], BF16)
            poolwB = Buf("poolw")
            for g in range(4):
                fw.dma("pool", wpl[:, g].rearrange("p k n -> p (k n)"), win_d[g], writes=[wplB[g]])
            fw.dma("pool", poolw[:].rearrange("p g d -> p (g d)"), poolw_d, writes=[poolwB])
            m_a2 = self.mark()

            NB = 4
            CH2 = 256
            xc = [self.sb("xc%d" % i, [128, KC, CH2], F32) for i in range(NB)]
            sqb = [self.sb("sqb%d" % i, [128, KC, CH2], BF16) for i in range(NB)]
            sr = [self.sb("sr%d" % i, [128, CH2], F32) for i in range(NB)]
            rs = sr
            xcB = [Buf() for _ in range(NB)]
            sqB = [Buf() for _ in range(NB)]
            srB = [Buf() for _ in range(NB)]
            rsB = srB
            def x_dma(c2):
                fw.dma("sp" if c2 % 2 == 0 else "act", xc[c2 % NB][:], xT[:, c2], writes=[xcB[c2 % NB]])
            for c2 in range(NB):
                x_dma(c2)
            for c2 in range(T // CH2):
                i = c2 % NB
                c = (c2 * CH2) // CH
                cs = slice(c2 * CH2, (c2 + 1) * CH2)
                self.act(sqb[i][:], xc[i][:], AF.Square, [xcB[i]], [sqB[i]])
                pb = c2 % 4
                for kc in range(KC):
                    self.mm(ps[pb][:, 0:CH2], ones, sqb[i][:, kc, :], kc == 0, kc == KC - 1,
                            [sqB[i], B_cstb], [psB[pb]], signal=(kc == KC - 1))
                self.act(sr[i][:], ps[pb][:, 0:CH2], AF.Ln, [psB[pb], B_cst], [srB[i]], scale=1.0 / D, bias=eps_ap)
                self.act(rs[i][:], sr[i][:], AF.Exp, [srB[i]], [rsB[i]], scale=-0.5)
                for kc in range(KC):
                    self.stt(xnT[:, kc, cs], xc[i][:, kc, :], self.cst("g1", kc, 1), rs[i][:],
                             ALU.mult, ALU.mult, [xcB[i], rsB[i], B_cst], [xnB[c]])
                nxt = c2 + NB
                if nxt < T // CH2 and nxt % 2 == 0:
                    x_dma(nxt)
                prv = c2 - 1 + NB
                if c2 >= 1 and prv < T // CH2 and prv % 2 == 1:
                    x_dma(prv)
            self.release(m_a2)
            if "xnT" in self.debug:
                d = ddbg("xnT", [128, KC, T])
                self.dump_bf16(d, xnT[:], [128, KC, T], xnB)

            if stop_after == "N1":
                self.finalize_stub(outT)
                return nc
            PADL = 16
            vpT = self.sb("vpT", [128, 4, PADL + T], F32)
            vpB = [Buf("vp%d" % g) for g in range(4)]
            pa = self.sb("pa", [128, PADL + T], F32)
            pbb = self.sb("pb", [128, PADL + T], F32)
            paB, pbB = Buf("pa"), Buf("pb")
            pbf = self.sb("pbf", [128, T], BF16)
            pbfB = Buf("pbf")
            ptmp = self.sb("ptmp", [128, 16], F32)
            ptmpB = Buf("ptmp")
            for g in range(4):
                self.memset("pool", vpT[:, g, 0:PADL], 0.0, [vpB[g]])
            bank = 0
            for g in range(4):
                for c in range(NCH):
                    pb = bank % 4
                    bank += 1
                    for kc in range(KC):
                        self.mm(ps[pb][:, :], wpl[:, g, kc, :], xnT[:, kc, c * CH:(c + 1) * CH], kc == 0, kc == KC - 1,
                                [wplB[g], xnB[c]], [psB[pb]], signal=(kc == KC - 1))
                    self.copy("act", vpT[:, g, PADL + c * CH:PADL + (c + 1) * CH], ps[pb][:, :], [psB[pb]], [vpB[g]])
            L = PADL + T
            self.memset("pool", pa[:, 0:PADL], 0.0, [paB])
            self.memset("pool", pbb[:, 0:PADL], 0.0, [pbB])
            self.n_t += 1
            wq_al = nc.alloc_sbuf_tensor_at("wq_al_%d" % self.n_t, [128, 8, KC, 128], BF16, offset=off_wout)
            self.n_t += 1
            csq_al = nc.alloc_sbuf_tensor_at("csq_al_%d" % self.n_t, [128, 2, T], F32, offset=off_mixT + 4 * T * 2)
            wqB = [Buf("wq%d" % i) for i in range(8)]
            csqB = Buf("csq")
            fw.dma("sp", csq_al[:], csq_d, writes=[csqB])
            for i in (0, 4, 1, 5, 2, 6, 3, 7):
                fw.dma("pool", wq_al[:, i].rearrange("p k n -> p (k n)"), win_d[4 + i], writes=[wqB[i]])
            evac_pending = []
            for g in (0, 1, 2, 3):
                w = (2, 4, 8, 16)[g]
                V = vpT[:, g, :]
                src, srcB = V, vpB[g]
                sh = 1
                weng, bufs = "dve", [(pa, paB), (pbb, pbB)]
                bi = 0
                while sh < w:
                    dst, dstB = bufs[bi]
                    bi ^= 1
                    self.tt(weng, dst[:, sh:L], src[:, sh:L], src[:, 0:L - sh], ALU.add, [srcB], [dstB])
                    src, srcB = dst, dstB
                    sh *= 2
                S = src
                while evac_pending:
                    evac_pending.pop(0)()
                self.stt(pbf[:, :], S[:, PADL:L], 1.0 / w, V[:, PADL:L], ALU.mult, ALU.subtract, [srcB, vpB[g]], [pbfB])
                nfix = w - 1
                self.tt("dve", ptmp[:, 0:nfix], S[:, PADL:PADL + nfix], self.cst("invc", g * 16, nfix), ALU.mult,
                        [srcB, B_cst], [ptmpB])
                self.tt("dve", pbf[:, 0:nfix], ptmp[:, 0:nfix], V[:, PADL:PADL + nfix], ALU.subtract,
                        [ptmpB, vpB[g]], [pbfB])
                for c in range(NCH):
                    pb = bank % 4
                    bank += 1
                    self.mm(ps[pb][:, :], poolw[:, g, :], pbf[:, c * CH:(c + 1) * CH], True, True,
                            [poolwB, pbfB], [psB[pb]])
                    evac_pending.append((lambda g=g, c=c, pb=pb: self.ts(
                        "dve", mixT[:, g, c * CH:(c + 1) * CH], ps[pb][:, :], self.cst("ps", g, 1), None,
                        ALU.mult, None, [psB[pb], B_cst], [mixB[g][c]])))
            while evac_pending:
                evac_pending.pop(0)()
            self.release(m_a)
            if "aout" in self.debug:
                d = ddbg("aout", [128, 4, T])
                self.dump_bf16(d, mixT[:, 0:4, :], [128, 4, T], [mixB[g][c] for g in range(4) for c in range(NCH)])

            if stop_after == "POOL":
                self.finalize_stub(outT)
                return nc
            cstab = self.sb("cstab", [128, 2, T], F32)
            cstabB = Buf("cstab")
            fw.dma("sp", cstab[:], csi_d, writes=[cstabB])
            wk = self.sb("wk", [128, 4, KC, 128], BF16)
            wkB = [Buf("wk%d" % i) for i in range(4)]
            wt = {}
            for i in range(8):
                wt[i] = ((lambda kc, i=i: wq_al[:, i, kc, :]), wqB[i], wq_al[:, i])
            for i in range(4):
                wt[8 + i] = ((lambda kc, i=i: wk[:, i, kc, :]), wkB[i], wk[:, i])
            widx = self.sb("widx", [128, 4, KC, 128], BF16)
            widxB = [Buf("widx%d" % i) for i in range(4)]
            iw = {i: ((lambda kc, i=i: widx[:, i, kc, :]), widxB[i], widx[:, i]) for i in range(4)}
            iw[4] = wt[0]
            iw[5] = wt[1]
            wvw = self.sb("wvw", [128, KC, 136], BF16)
            wvwB = Buf("wvw")
            pend = []
            for i in (8, 10, 9, 11):
                pend.append((lambda i=i: fw.dma("pool", wt[i][2].rearrange("p k n -> p (k n)"), win_d[4 + i],
                                                writes=[wt[i][1]])))
            for i in (0, 2, 1, 3):
                pend.append((lambda i=i: fw.dma("pool", iw[i][2].rearrange("p k n -> p (k n)"), win_d[16 + i],
                                                writes=[iw[i][1]])))
            pend.append(lambda: fw.dma("pool", wvw[:].rearrange("p k n -> p (k n)"), wvw_d, writes=[wvwB]))
            for _ in range(2):
                pend.pop(0)()
            NTB = 2
            t1 = [self.sb("t1_%d" % i, [128, CH], F32) for i in range(NTB)]
            t2 = [self.sb("t2_%d" % i, [128, CH], F32) for i in range(NTB)]
            sq2 = [self.sb("sq2_%d" % i, [128, CH], BF16) for i in range(NTB)]
            s2 = [self.sb("s2_%d" % i, [128, CH], F32) for i in range(NTB)]
            t1B = [Buf() for _ in range(NTB)]
            t2B = [Buf() for _ in range(NTB)]
            sq2B = [Buf() for _ in range(NTB)]
            s2B = [Buf() for _ in range(NTB)]
            r2, r2B = s2, s2B
            t3, t3B = t1, t1B
            it = 0
            groups = []
            for p in range(4):
                groups.append((p, 4 + p, "gq", "gqp", (lambda c, p=p: qT[:, p, c * CH:(c + 1) * CH]), qB))
            for g in range(2):
                groups.append((8 + g, 10 + g, "gk", "gkp", g, kB))
            for g in range(2):
                self.memset("pool", kTz[64:128, g, 0, :], 0.0, kB)
                self.memset("pool", kTz[0:64, g, 1, :], 0.0, kB)
            for c in range(NCH):
                cs = slice(c * CH, (c + 1) * CH)
                for (ri, pi, gn, gpn, dst, dB) in groups:
                    i = it % NTB
                    it += 1
                    pr, pp, pq = 0 + 3 * (it % 2), 1 + 3 * (it % 2), 2 + 3 * (it % 2)
                    for kc in range(KC):
                        self.mm(ps[pr][:, :], wt[ri][0](kc), xnT[:, kc, cs], kc == 0, kc == KC - 1,
                                [wt[ri][1], xnB[c]], [psB[pr]], signal=(kc == KC - 1))
                    for kc in range(KC):
                        self.mm(ps[pp][:, :], wt[pi][0](kc), xnT[:, kc, cs], kc == 0, kc == KC - 1,
                                [wt[pi][1], xnB[c]], [psB[pp]], signal=(kc == KC - 1))
                    self.act(sq2[i][:], ps[pr][:, :], AF.Square, [psB[pr]], [sq2B[i]])
                    self.mm(ps[pq][:, :], blockones, sq2[i][:], True, True, [B_cstb, sq2B[i]], [psB[pq]])
                    self.stt(t1[i][:], ps[pr][:, :], self.cst(gn), csq_al[:, 0, cs], ALU.mult, ALU.mult,
                             [psB[pr], B_cst, csqB], [t1B[i]])
                    self.stt(t2[i][:], ps[pp][:, :], self.cst(gpn), csq_al[:, 1, cs], ALU.mult, ALU.mult,
                             [psB[pp], B_cst, csqB], [t2B[i]])
                    self.tt("dve", t3[i][:], t1[i][:], t2[i][:], ALU.add, [t1B[i], t2B[i]], [t3B[i]])
                    self.act(s2[i][:], ps[pq][:, :], AF.Ln, [psB[pq], B_cst], [s2B[i]], scale=1.0 / 64, bias=eps_ap)
                    self.act(r2[i][:], s2[i][:], AF.Exp, [s2B[i]], [r2B[i]], scale=-0.5)
                    if callable(dst):
                        self.tt("dve", dst(c), t3[i][:], r2[i][:], ALU.mult, [t3B[i], r2B[i]], [dB[c]])
                    else:
                        g_ = dst
                        self.tt("dve", kTz[0:64, g_, 0, cs], t3[i][0:64, :], r2[i][0:64, :], ALU.mult, [t3B[i], r2B[i]], [dB[c]])
                        self.tt("dve", kTz[64:128, g_, 1, cs], t3[i][64:128, :], r2[i][64:128, :], ALU.mult,
                                [t3B[i], r2B[i]], [dB[c]])
                    for _ in range(2):
                        if pend:
                            pend.pop(0)()
            if "qT" in self.debug:
                d = ddbg("qT", [128, 4, T])
                self.dump_bf16(d, qT[:], [128, 4, T], qB)
                d = ddbg("kT2", [128, 4, T])
                self.dump_bf16(d, kTz[:].rearrange("p g r t -> p (g r) t"), [128, 4, T], kB)

            if stop_after == "QK":
                self.finalize_stub(outT)
                return nc
            while pend:
                pend.pop(0)()
            for i in (4, 5):
                fw.dma("pool", iw[i][2].rearrange("p k n -> p (k n)"), win_d[16 + i], writes=[iw[i][1]])
            for c in range(NCH):
                self.memset("pool", vxe[:, 4 * c:4 * c + 4, :, 64:128], 1.0, [vB[c]])
                self.memset("pool", vxo[:, 4 * c:4 * c + 4, :, 0:64], 1.0, [vB[c]])
            for tt_ in range(16):
                c = tt_ // 4
                pb = 6 + (tt_ % 2)
                for kc in range(KC):
                    self.mm(ps[pb][:, 0:136], xnT[:, kc, tt_ * 128:(tt_ + 1) * 128], wvw[:, kc, :], kc == 0, kc == KC - 1,
                            [wvwB, xnB[c]], [psB[pb]], signal=(kc == KC - 1))
                vsrc = ps[pb][:, 0:128].rearrange("p (g d) -> p g d", g=2)
                self.copy("act", vxe[:, tt_, :, 0:64], vsrc, [psB[pb]], [vB[c]])
                self.copy("dve", vxo[:, tt_, :, 64:128], vsrc, [psB[pb]], [vB[c]])
                self.ts("dve", wts[:, tt_, :], ps[pb][:, 128:136], 0.0625, None, ALU.mult, None, [psB[pb]], [wB[c]])
            igroups = [(0, 2, (lambda c: qiT[:, 0, c * CH:(c + 1) * CH]), qiB),
                       (1, 3, (lambda c: qiT[:, 1, c * CH:(c + 1) * CH]), qiB),
                       (4, 5, (lambda c: kiT[:, c * CH:(c + 1) * CH]), kiB)]
            for c in range(NCH):
                cs = slice(c * CH, (c + 1) * CH)
                for (ri, pi, dst, dB) in igroups:
                    i = it % NTB
                    it += 1
                    pr, pp = 0 + 3 * (it % 2), 1 + 3 * (it % 2)
                    for kc in range(KC):
                        self.mm(ps[pr][:, :], iw[ri][0](kc), xnT[:, kc, cs], kc == 0, kc == KC - 1,
                                [iw[ri][1], xnB[c]], [psB[pr]], signal=(kc == KC - 1))
                    for kc in range(KC):
                        self.mm(ps[pp][:, :], iw[pi][0](kc), xnT[:, kc, cs], kc == 0, kc == KC - 1,
                                [iw[pi][1], xnB[c]], [psB[pp]], signal=(kc == KC - 1))
                    self.tt("dve", t1[i][:], ps[pr][:, :], cstab[:, 0, cs], ALU.mult, [psB[pr], cstabB], [t1B[i]])
                    self.tt("dve", t2[i][:], ps[pp][:, :], cstab[:, 1, cs], ALU.mult, [psB[pp], cstabB], [t2B[i]])
                    self.tt("dve", dst(c), t1[i][:], t2[i][:], ALU.add, [t1B[i], t2B[i]], [dB[c]])
            self.release(m_ab)
            if "idx" in self.debug:
                d = ddbg("qiT", [128, 2, T])
                self.dump_bf16(d, qiT[:], [128, 2, T], qiB)
                d = ddbg("kiT", [128, 1, T])
                self.dump_bf16(d, kiT[:].rearrange("p (o t) -> p o t", o=1), [128, 1, T], kiB)
                d = ddbg("vxe", [128, 32, 128])
                self.dump_bf16(d, vxe[:].rearrange("p a g d -> p (a g) d"), [128, 32, 128], vB)
                d = ddbg("wts", [128, 16, 8])
                fw.dma("sp", d, wts[:], reads=wB)

            if stop_after == "A":
                self.finalize_stub(outT)
                return nc


            fw.barrier()
            fw.dma("pool", wout_sb[:].rearrange("p k n -> p (k n)"), wout_d, writes=[B_wout])
            MNEG = -30000.0
            score = self.sb("score", [128, 4, T], F32)
            scB = [Buf("sc%d" % b) for b in range(4)]
            NRB = 3
            rbuf = [self.sb("rbuf%d" % i, [128, CH], F32) for i in range(NRB)]
            rbB = [Buf() for _ in range(NRB)]
            maskq = self.sb("maskq", [128, 4, T], BF16)
            mqB = [Buf("mq%d" % b) for b in range(4)]
            junk = maskq[:, 0, :]
            junkB = mqB[0]
            maskT = [self.sb("maskT%d" % i, [128, 16, CH], BF16) for i in range(2)]
            mTB = [[Buf("mT%d_%d" % (i, j)) for j in range(16)] for i in range(2)]
            NEB = 5
            ebuf = [self.sb("ebuf%d" % i, [128, CH], BF16) for i in range(NEB)]
            ebB = [Buf() for _ in range(NEB)]
            rden = [self.sb("rden%d" % i, [128, CH], F32) for i in range(2)]
            rdB = [Buf() for _ in range(2)]
            NH = NITER + 2
            Rv = self.sb("Rv", [128, 4], F32)
            Rp = self.sb("Rp", [128, 4], F32)
            Rn = self.sb("Rn", [128, 4], F32)
            RnB = Buf("Rn")
            halfs = self.sb("halfs", [128, 4, NH], F32)
            mid = self.sb("mid", [128, 4], F32)
            midp = self.sb("midp", [128, 4], F32)
            tsel = self.sb("tsel", [128, 4], F32)
            cnt = self.sb("cnt", [128, 4], F32)
            thr_all = self.sb("thr_all", [128, 16], F32)
            RvB, RpB, hfB, midB, midpB, tselB, cntB, thrB = (Buf("Rv"), Buf("Rp"), Buf("halfs"), Buf("mid"),
                                                            Buf("midp"), Buf("tsel"), Buf("cnt"), Buf("thr"))
            stB = {"ibank": 0, "ri": 0, "ei": 0, "sbank": 0}
            COL = {0: 0, 1: 1, 2: 2, 3: 3}
            junk2 = maskq[:, 1, :]
            junk2B = mqB[1]
            sacc = self.sb("sacc", [128, 4], F32)
            saccB = Buf("sacc")

            def units_I(c):
                us = []
                for b in range(4):
                    i = 4 * c + b
                    qs = slice(i * 128, (i + 1) * 128)
                    for kc2 in range(c + 1):
                        def u(b=b, i=i, qs=qs, kc2=kc2):
                            n = CH if kc2 < c else 128 * (b + 1)
                            ks = slice(kc2 * CH, kc2 * CH + n)
                            for h in range(8):
                                pb = 4 + stB["ibank"] % 4
                                stB["ibank"] += 1
                                r0 = 32 * (h % 4)
                                self.mm(ps[pb][:, 0:n], qiT[r0:r0 + 32, h // 4, qs], kiT[r0:r0 + 32, ks], True, True,
                                        [qiB[c], kiB[kc2]], [psB[pb]], tp=(r0, 0))
                                rb = stB["ri"] % NRB
                                stB["ri"] += 1
                                self.act(rbuf[rb][:, 0:n], ps[pb][:, 0:n], AF.Relu, [psB[pb]], [rbB[rb]])
                                if h == 0:
                                    self.ts("dve", score[:, b, ks], rbuf[rb][:, 0:n], wts[:, i, 0:1], None, ALU.mult, None,
                                            [rbB[rb], wB[c]], [scB[b]])
                                else:
                                    self.stt(score[:, b, ks], rbuf[rb][:, 0:n], wts[:, i, h:h + 1], score[:, b, ks],
                                             ALU.mult, ALU.add, [rbB[rb], wB[c], scB[b]], [scB[b]])
                        us.append(u)

                    def u2(b=b, i=i):
                        ncol = 128 * (i + 1)
                        cb_ = COL[b]
                        self.ts("dve", junk[:, 0:ncol], score[:, b, 0:ncol], 1.0, -3.0e38, ALU.mult, ALU.max,
                                [scB[b]], [junkB, RvB], accum_out=Rv[:, cb_:cb_ + 1])
                        self.ts("dve", junk[:, 0:ncol], score[:, b, 0:ncol], -1.0, -3.0e38, ALU.mult, ALU.max,
                                [scB[b]], [junkB, RnB], accum_out=Rn[:, cb_:cb_ + 1])
                        self.tt("dve", score[:, b, i * 128:(i + 1) * 128], score[:, b, i * 128:(i + 1) * 128],
                                self.cst("cmask"), ALU.add, [scB[b], B_cst], [scB[b]])
                    us.append(u2)
                return us

            def units_S(c):
                us = []
                mb = c % 2

                def u0():
                    self.tt("dve", Rv[:, :], Rv[:, :], Rn[:, :], ALU.max, [RvB, RnB], [RvB])
                    self.ts("dve", Rp[:, :], Rv[:, :], 1.001, 1e-6, ALU.mult, ALU.add, [RvB], [RpB])
                    for b in range(4):
                        self.ts("dve", halfs[:, b, :], self.cst("pw2"), Rp[:, b:b + 1], None, ALU.mult, None,
                                [RpB, B_cst], [hfB])
                    self.memset("dve", mid[:, :], 0.0, [midB])
                us.append(u0)
                for k in range(1, NITER + 1):
                    def uk(k=k):
                        if k < NITER:
                            Hs, H2 = halfs[:, :, k], halfs[:, :, k - 1]
                        else:
                            Hs, H2 = halfs[:, :, k - 1], halfs[:, :, k - 1]
                        act_tiles = (2, 3) if (c == 3 or k % 2 == 0) else (3,)
                        dve_tiles = tuple(b for b in range(4) if b not in act_tiles)
                        for b in act_tiles:
                            ncol = 128 * (4 * c + b + 1)
                            cb_ = COL[b]
                            self.act(junk2[:, 0:ncol], score[:, b, 0:ncol], AF.Sign, [scB[b], midB], [junk2B, saccB],
                                     scale=-1.0, bias=mid[:, cb_:cb_ + 1], accum_out=sacc[:, cb_:cb_ + 1])
                        self.tt("dve", midp[:, :], mid[:, :], Hs, ALU.subtract, [midB, hfB], [midpB])
                        for b in dve_tiles:
                            ncol = 128 * (4 * c + b + 1)
                            cb_ = COL[b]
                            self.ts("dve", junk[:, 0:ncol], score[:, b, 0:ncol], mid[:, cb_:cb_ + 1], 0.0, ALU.is_ge, ALU.add,
                                    [scB[b], midB], [junkB, cntB], accum_out=cnt[:, cb_:cb_ + 1])
                        nd = len(dve_tiles)
                        self.stt(tsel[:, 0:nd], cnt[:, 0:nd], float(TOPK), H2[:, 0:nd], ALU.is_ge, ALU.mult, [cntB, hfB], [tselB])
                        for b in act_tiles:
                            ncol = 128 * (4 * c + b + 1)
                            cb_ = COL[b]
                            self.stt(tsel[:, cb_:cb_ + 1], sacc[:, cb_:cb_ + 1], float(ncol - 2 * TOPK), H2[:, cb_:cb_ + 1],
                                     ALU.is_le, ALU.mult, [saccB, hfB], [tselB])
                        self.tt("dve", mid[:, :], midp[:, :], tsel[:, :], ALU.add, [midpB, tselB], [midB])
                    us.append(uk)

                def um():
                    for a_ in range(1, 4):
                        self.memset("pool", maskT[mb][:, 4 * c + a_, 0:a_ * 128], MNEG, [mTB[mb][4 * c + a_]])
                    for b in range(4):
                        ncol = 128 * (4 * c + b + 1)
                        cb_ = COL[b]
                        self.copy("dve", thr_all[:, 4 * c + b:4 * c + b + 1], mid[:, cb_:cb_ + 1], [midB], [thrB])
                        self.ts("dve", maskq[:, b, 0:ncol], score[:, b, 0:ncol], mid[:, cb_:cb_ + 1], MNEG, ALU.is_lt, ALU.mult,
                                [scB[b], midB], [mqB[b]])
                us.append(um)
                for j in range(4 * c + 4):
                    def ut(j=j):
                        b0 = max(0, j - 4 * c)
                        tb = 0
                        for b in range(b0, 4):
                            self.tr(psT[tb][:, b * 128:(b + 1) * 128], maskq[:, b, j * 128:(j + 1) * 128], ident,
                                    [mqB[b], B_cstb], [psTB[tb]], signal=(b == 3))
                        self.copy("dve", maskT[mb][:, j, b0 * 128:CH], psT[tb][:, b0 * 128:CH], [psTB[tb]], [mTB[mb][j]])
                    us.append(ut)
                return us

            def units_A(c):
                us = []
                cs = slice(c * CH, (c + 1) * CH)
                mb = c % 2
                nj = 4 * c + 4
                norm_pending = []
                for h in range(8):
                    g, p, base = h // 4, h // 2, 64 * (h % 2)
                    vx = vxe if h % 2 == 0 else vxo
                    accb = 2 + (h % 2)
                    hstate = {"pend": []}
                    DEPTH = 2
                    for step in range(nj + DEPTH):
                        def ustep(step=step, h=h, g=g, p=p, base=base, vx=vx, accb=accb, hstate=hstate):
                            if step < nj:
                                j = step
                                sb_ = stB["sbank"] % 2
                                stB["sbank"] += 1
                                q0 = 128 * max(0, j - 4 * c)
                                qcs = slice(c * CH + q0, (c + 1) * CH)
                                self.mm(ps[sb_][:, q0:CH], kTz[:, g, h % 2, j * 128:(j + 1) * 128], qT[:, p, qcs],
                                        True, False, [kB[j // 4], qB[c]], [psB[sb_]], signal=False)
                                self.mm(ps[sb_][:, q0:CH], ident, maskT[mb][:, j, q0:CH], False, True, [B_cstb, mTB[mb][j]],
                                        [psB[sb_]])
                                e = stB["ei"] % NEB
                                stB["ei"] += 1
                                self.act(ebuf[e][:, q0:CH], ps[sb_][:, q0:CH], AF.Exp, [psB[sb_]], [ebB[e]], scale=0.125)
                                hstate["pend"].append((j, e, q0))
                            if step >= DEPTH:
                                pj, pe_, pq0 = hstate["pend"].pop(0)
                                self.mm(ps[accb][:, pq0:CH], vx[:, pj, g, :], ebuf[pe_][:, pq0:CH], pj == 0, pj == nj - 1,
                                        [vB[pj // 4], ebB[pe_]], [psB[accb]], signal=True)
                        us.append(ustep)

                    def unorm_act(h=h, p=p, accb=accb):
                        rd = h % 2
                        rows = slice(64, 128) if h % 2 == 0 else slice(0, 64)
                        self.act(rden[rd][rows, :], ps[accb][rows, :], AF.Ln, [psB[accb]], [rdB[rd]])
                        self.act(rden[rd][rows, :], rden[rd][rows, :], AF.Exp, [rdB[rd]], [rdB[rd]], scale=-1.0)

                    def unorm_dve(h=h, p=p, accb=accb):
                        rd = h % 2
                        if h % 2 == 0:
                            self.tt("dve", mixT[0:64, 4 + p, cs], ps[accb][0:64, :], rden[rd][64:128, :], ALU.mult,
                                    [psB[accb], rdB[rd]], [mixB[4 + p][c]])
                        else:
                            self.tt("dve", mixT[64:128, 4 + p, cs], ps[accb][64:128, :], rden[rd][0:64, :], ALU.mult,
                                    [psB[accb], rdB[rd]], [mixB[4 + p][c]])
                    us.append(unorm_act)
                    if norm_pending:
                        us.append(norm_pending.pop(0))
                    norm_pending.append(unorm_dve)
                us.extend(norm_pending)
                return us

            def interleave(la, lb):
                na, nb = len(la), len(lb)
                ia = ib = 0
                while ia < na or ib < nb:
                    if ib >= nb or (ia < na and ia * nb <= ib * na):
                        la[ia]()
                        ia += 1
                    else:
                        lb[ib]()
                        ib += 1

            order = [3, 2, 1, 0]
            for u in units_I(order[0]) + units_S(order[0]):
                u()
            for oi, c in enumerate(order):
                la = units_A(c)
                lb = (units_I(order[oi + 1]) + units_S(order[oi + 1])) if oi + 1 < NCH else []
                interleave(la, lb)
            if "bout" in self.debug:
                d = ddbg("thr", [128, 16])
                fw.dma("sp", d, thr_all[:], reads=[thrB])
                d = ddbg("bout", [128, 4, T])
                self.dump_bf16(d, mixT[:, 4:8, :], [128, 4, T], [mixB[4 + p][c] for p in range(4) for c in range(NCH)])
            fw.barrier()
            self.release(m_persist)
            if stop_after == "B":
                self.finalize_stub(outT)
                return nc

            x1T = self.sbtop("x1T", [128, KC, T], F32)
            x1B = [[Buf("x1_%d_%d" % (n, c)) for c in range(NCH)] for n in range(KC)]
            HT = 1024
            hnT = self.sbtop("hnT", [128, KC, HT], BF16)
            hnB = [Buf("hn0"), Buf("hn1")]

            def norm2_cc(c, cc, pb, sq3, sq3Bs, sr3t, sr3Bs):
                cs = slice(c * CH, (c + 1) * CH)
                hs = slice(cc * CH, (cc + 1) * CH)
                self.act(sq3[:], x1T[:, :, cs], AF.Square, [x1B[n][c] for n in range(KC)], sq3Bs)
                for kc in range(KC):
                    self.mm(ps[pb][:, :], ones, sq3[:, kc, :], kc == 0, kc == KC - 1, sq3Bs + [B_cstb], [psB[pb]],
                            signal=(kc == KC - 1))
                self.act(sr3t[:, 0, :], ps[pb][:, :], AF.Ln, [psB[pb], B_cst], sr3Bs, scale=1.0 / D, bias=eps_ap)
                self.act(sr3t[:, 1, :], sr3t[:, 0, :], AF.Exp, sr3Bs, sr3Bs, scale=-0.5)
                for kc in range(KC):
                    self.stt(hnT[:, kc, hs], x1T[:, kc, cs], self.cst("g2", kc, 1), sr3t[:, 1, :], ALU.mult, ALU.mult,
                             [x1B[kc][c], B_cst] + sr3Bs, [hnB[cc]])
            NWU = 3
            wu = [self.sbtop("wu%d" % i, [128, 2, KC, 128], BF16) for i in range(NWU)]
            wuB = [Buf() for _ in range(NWU)]
            sq3c = self.sb("sq3c", [128, KC, CH], BF16)
            sr3c = self.sb("sr3c", [128, 2, CH], F32)
            sq3cB, sr3cB = [Buf("sq3c")], [Buf("sr3c")]
            NXR = 2
            xres = [self.sb("xres%d" % i, [128, 2, KC, 256], F32) for i in range(NXR)]
            xrB = [[Buf(), Buf()] for _ in range(NXR)]

            def xr_dma(c):
                for a in range(2):
                    fw.dma("sp" if a == 0 else "act", xres[c % NXR][:, a], xT[:, 2 * c + a], writes=[xrB[c % NXR][a]])
            for c in range(NXR):
                xr_dma(c)
            cnt_c = 0
            for c in range(NCH):
                cs = slice(c * CH, (c + 1) * CH)
                xi = c % NXR
                for n in range(KC):
                    if c in (1, 2) and n == 2:
                        norm2_cc(c - 1, c - 1, cnt_c % 8, sq3c, sq3cB, sr3c, sr3cB)
                        cnt_c += 1
                    pb = cnt_c % 8
                    cnt_c += 1
                    for ct in range(8):
                        self.mm(ps[pb][:, :], wout_sb[:, ct, n * 128:(n + 1) * 128], mixT[:, ct, cs], ct == 0, ct == 7,
                                [B_wout, mixB[ct][c]], [psB[pb]], signal=(ct == 7))
                    self.tt("dve", x1T[:, n, cs].rearrange("p (a t) -> p a t", a=2),
                            ps[pb][:, :].rearrange("p (a t) -> p a t", a=2), xres[xi][:, :, n, :], ALU.add,
                            [psB[pb], xrB[xi][0], xrB[xi][1]], [x1B[n][c]])
                if c + NXR < NCH:
                    xr_dma(c + NXR)
                if c == 1:
                    for i in range(2):
                        fw.dma("pool", wu[i][:].rearrange("p g k n -> p (g k n)"), wup_d[i], writes=[wuB[i]])
            if "x1" in self.debug:
                d = ddbg("x1", [128, KC, T])
                for n in range(KC):
                    fw.dma("sp", d[:, n, :], x1T[:, n, :], reads=x1B[n])
            self.release(m_const)
            if stop_after == "C":
                self.finalize_stub(outT)
                return nc

            actT = self.sb("actT", [128, NFT, HT], BF16)
            actB = [Buf("act%d" % i) for i in range(NFT)]
            NUB = 3
            Ug = [self.sb("Ug%d" % i, [128, 2 + HT], F32) for i in range(NUB)]
            Uv = [self.sb("Uv%d" % i, [128, 2 + HT], F32) for i in range(NUB)]
            UgB = [Buf() for _ in range(NUB)]
            UvB = [Buf() for _ in range(NUB)]
            NCB = 3
            cgv = []
            offs_cgv = []
            for i in range(NCB):
                offs_cgv.append(self.sb_off)
                cgv.append(self.sb("cgv%d" % i, [128, 2, HT], F32))
            cg = [cgv[i][:, 0, :] for i in range(NCB)]
            cv = [cgv[i][:, 1, :] for i in range(NCB)]
            cgB = [Buf() for _ in range(NCB)]
            cvB = [Buf() for _ in range(NCB)]
            self.n_t += 1
            sq3 = nc.alloc_sbuf_tensor_at("sq3_%d" % self.n_t, [128, KC, CH], BF16, offset=offs_cgv[NCB - 1])
            self.n_t += 1
            sr3t = nc.alloc_sbuf_tensor_at("sr3_%d" % self.n_t, [128, 2, CH], F32, offset=offs_cgv[NCB - 2])
            sq3Bs = [cgB[NCB - 1], cvB[NCB - 1]]
            sr3Bs = [cgB[NCB - 2]]
            halo = self.sb("halo", [128, 2 * NFT, 2], F32)
            haloB = [Buf() for _ in range(2 * NFT)]
            NWD = 2
            wd = [self.sb("wd%d" % i, [128, NFT, 128], BF16) for i in range(NWD)]
            wdB = [Buf() for _ in range(NWD)]
            ost = [self.sb("ost%d" % i, [128, CH], F32) for i in range(2)]
            ostB = [Buf() for _ in range(2)]
            out_toks = []
            st_ = {"bank": 0, "osi": 0}
            fw.max_pool_outstanding = 6

            def load_wu(hf, i):
                w_ = (hf * NFT + i) % NWU
                fw.dma("pool", wu[w_][:].rearrange("p g k n -> p (g k n)"), wup_d[i], writes=[wuB[w_]])

            def load_wd(hf, n):
                d_ = (hf * KC + n) % NWD
                fw.dma("pool", wd[d_][:].rearrange("p f m -> p (f m)"), wdn_d[n], writes=[wdB[d_]])

            def stage0(hf, i):
                w_ = (hf * NFT + i) % NWU
                ub = (hf * NFT + i) % NUB
                cb = (hf * NFT + i) % NCB
                for (gi, U, UB, fidx) in ((0, Ug[ub], UgB[ub], i), (1, Uv[ub], UvB[ub], NFT + i)):
                    for cc in range(2):
                        pb = st_["bank"] % 8
                        st_["bank"] += 1
                        for kc in range(KC):
                            self.mm(ps[pb][:, :], wu[w_][:, gi, kc, :], hnT[:, kc, cc * CH:(cc + 1) * CH],
                                    kc == 0, kc == KC - 1, [wuB[w_], hnB[cc]], [psB[pb]], signal=(kc == KC - 1))
                        self.copy("act", U[:, 2 + cc * CH:2 + (cc + 1) * CH], ps[pb][:, :], [psB[pb]], [UB])
                        c0, c0B = (cg[cb], cgB[cb]) if gi == 0 else (cv[cb], cvB[cb])
                        self.act(c0[:, cc * CH:(cc + 1) * CH], ps[pb][:, :], AF.Identity, [psB[pb], B_cst], [c0B],
                                 scale=self.cst("cw", 2 * 2 * NFT + fidx, 1), bias=self.cst("cb", fidx, 1))

            def halo_in(hf, i):
                ub = (hf * NFT + i) % NUB
                for (U, UB, fidx) in ((Ug[ub], UgB[ub], i), (Uv[ub], UvB[ub], NFT + i)):
                    if hf == 0:
                        if i < NUB:
                            self.memset("dve", U[:, 0:2], 0.0, [UB])
                    else:
                        self.copy("dve", U[:, 0:2], halo[:, fidx, :], [haloB[fidx]], [UB])

            def stage1(hf, i):
                ub = (hf * NFT + i) % NUB
                cb = (hf * NFT + i) % NCB
                for (U, UB, fidx, cc_, ccB) in ((Ug[ub], UgB[ub], i, cg[cb], cgB[cb]),
                                                (Uv[ub], UvB[ub], NFT + i, cv[cb], cvB[cb])):
                    if hf == 0:
                        self.copy("dve", halo[:, fidx, :], U[:, HT:HT + 2], [UB], [haloB[fidx]])
                    w0 = self.cst("cw", 0 * 2 * NFT + fidx, 1)
                    w1 = self.cst("cw", 1 * 2 * NFT + fidx, 1)
                    self.stt(cc_, U[:, 1:1 + HT], w1, cc_, ALU.mult, ALU.add, [UB, B_cst, ccB], [ccB])
                    self.stt(cc_, U[:, 0:HT], w0, cc_, ALU.mult, ALU.add, [UB, B_cst, ccB], [ccB])

            def stage2(hf, i):
                cb = (hf * NFT + i) % NCB
                self.act(cg[cb], cg[cb], AF.Silu, [cgB[cb]], [cgB[cb]])
                self.tt("dve", actT[:, i, :], cg[cb], cv[cb], ALU.mult, [cgB[cb], cvB[cb]], [actB[i]])

            def norm2(hf):
                for cc in range(2):
                    pb = st_["bank"] % 8
                    st_["bank"] += 1
                    norm2_cc(2 * hf + cc, cc, pb, sq3, sq3Bs, sr3t, sr3Bs)

            for hf in range(2):
                for it in range(NFT + 2):
                    if it < NFT:
                        halo_in(hf, it)
                    if it >= 2:
                        stage2(hf, it - 2)
                    if it < NFT:
                        if it + 2 < NFT:
                            load_wu(hf, it + 2)
                        stage0(hf, it)
                    if 1 <= it <= NFT:
                        stage1(hf, it - 1)
                    if it == NFT - 1:
                        load_wd(hf, 0)
                for n in range(KC):
                    d_ = (hf * KC + n) % NWD
                    if n + 1 < KC:
                        load_wd(hf, n + 1)
                    if hf == 0 and n == 1:
                        norm2(1)
                        load_wu(1, 0)
                        load_wu(1, 1)
                    for cc in range(2):
                        c = 2 * hf + cc
                        cs = slice(c * CH, (c + 1) * CH)
                        pb = st_["bank"] % 8
                        st_["bank"] += 1
                        for ft in range(NFT):
                            self.mm(ps[pb][:, :], wd[d_][:, ft, :], actT[:, ft, cc * CH:(cc + 1) * CH], ft == 0, ft == NFT - 1,
                                    [wdB[d_], actB[ft]], [psB[pb]], signal=(ft == NFT - 1))
                        o_ = st_["osi"] % 2
                        st_["osi"] += 1
                        self.tt("dve", ost[o_][:], ps[pb][:, :], x1T[:, n, cs], ALU.add, [psB[pb], x1B[n][c]], [ostB[o_]])
                        out_toks.append(fw.dma("sp", outT[:, n, cs], ost[o_][:], reads=[ostB[o_]]))
            fw.barrier()
            fw.finish()
            return nc

    def dump_bf16(self, d, src_ap, shape, bufs):
        fw = self.fw
        m = self.mark()
        n1 = shape[1]
        stg = self.sb("dbgstg", [128, shape[2]], F32)
        sB = Buf("dbgstg")
        for a in range(n1):
            self.copy("dve", stg[:, :], src_ap[:, a, :], list(bufs), [sB])
            fw.dma("sp", d[:, a, :], stg[:, :], reads=[sB])
        fw.barrier()
        self.release(m)

    def finalize_stub(self, outT):
        fw = self.fw
        m = self.mark()
        z = self.sb("zstub", [128, 512], F32)
        zB = Buf()
        self.memset("dve", z[:], 0.0, [zB])
        for kc in range(KC):
            for c in range(NCH):
                fw.dma("sp", outT[:, kc, c * CH:(c + 1) * CH], z[:], reads=[zB])
        fw.barrier()
        fw.finish()


def _rope_tables(head_dim, reps):
    half = head_dim // 2
    inv = np.exp(-np.log(np.float32(10000.0)) * np.arange(half, dtype=np.float32) / half).astype(np.float32)
    pos = np.arange(T, dtype=np.float32)
    ang = (pos[:, None] * inv[None, :]).astype(np.float32)
    cos = np.cos(ang).astype(np.float32).T
    sin = np.sin(ang).astype(np.float32).T
    cos_full = np.concatenate([cos, cos], 0)
    sin_signed = np.concatenate([-sin, sin], 0)
    tab = np.stack([np.tile(cos_full, (reps, 1)), np.tile(sin_signed, (reps, 1))], 1)
    return np.ascontiguousarray(tab.astype(np.float32))


def _perm(head_dim):
    half = head_dim // 2
    return np.concatenate([np.arange(half, head_dim), np.arange(0, half)])


def _tile_k(w):
    n = w.shape[1]
    return np.ascontiguousarray(w.reshape(KC, 128, n).transpose(1, 0, 2).reshape(128, KC * n))


def host_prepare(inp):
    f = np.float32
    w_in = np.asarray(inp["w_in"][0], f)
    cuts = np.cumsum([512, 512, 128, 128, 256, 32, 8])
    w_vp = w_in[:, 0:cuts[0]]
    w_q = w_in[:, cuts[0]:cuts[1]]
    w_k = w_in[:, cuts[1]:cuts[2]]
    w_v = w_in[:, cuts[2]:cuts[3]]
    w_qi = w_in[:, cuts[3]:cuts[4]]
    w_ki = w_in[:, cuts[4]:cuts[5]]
    w_wi = w_in[:, cuts[5]:cuts[6]]
    p64 = _perm(64)
    p32 = _perm(32)
    tiles = []
    for g in range(4):
        tiles.append(w_vp[:, g * 128:(g + 1) * 128])
    wq_h = w_q.reshape(D, 8, 64)
    for p in range(4):
        tiles.append(wq_h[:, 2 * p:2 * p + 2, :].reshape(D, 128))
    for p in range(4):
        tiles.append(wq_h[:, 2 * p:2 * p + 2, :][:, :, p64].reshape(D, 128))
    wk_h = w_k.reshape(D, 2, 64)
    for g in range(2):
        tiles.append(np.concatenate([wk_h[:, g, :], wk_h[:, g, :]], 1))
    for g in range(2):
        tiles.append(np.concatenate([wk_h[:, g, :][:, p64], wk_h[:, g, :][:, p64]], 1))
    wqi_h = w_qi.reshape(D, 8, 32)
    for a in range(2):
        tiles.append(wqi_h[:, 4 * a:4 * a + 4, :].reshape(D, 128))
    for a in range(2):
        tiles.append(wqi_h[:, 4 * a:4 * a + 4, :][:, :, p32].reshape(D, 128))
    tiles.append(np.tile(w_ki, (1, 4)))
    tiles.append(np.tile(w_ki[:, p32], (1, 4)))
    assert len(tiles) == NWIN
    win = np.stack([_tile_k(t) for t in tiles], 0)
    wvw = _tile_k(np.concatenate([w_v, w_wi], 1))
    poolw = np.ascontiguousarray(np.asarray(inp["pool_w"][0], f).transpose(1, 0, 2).reshape(128, 4 * 128))
    wout = _tile_k(np.asarray(inp["w_out"][0], f))
    w_up = np.asarray(inp["w_up"][0], f)
    wup = np.stack([np.concatenate([_tile_k(w_up[:, t * 128:(t + 1) * 128]),
                                    _tile_k(w_up[:, (NFT + t) * 128:(NFT + t + 1) * 128])], 1) for t in range(NFT)], 0)
    w_dn = np.asarray(inp["w_down"][0], f)
    wdn = np.stack([np.ascontiguousarray(w_dn[:, n * 128:(n + 1) * 128].reshape(NFT, 128, 128)
                                         .transpose(1, 0, 2).reshape(128, NFT * 128)) for n in range(8)], 0)
    cst = np.zeros((128, NCST), f)

    def put(name, arr):
        o, n = CST[name]
        cst[:, o:o + n] = np.asarray(arr, f).reshape(128, n)
    put("g1", np.asarray(inp["norm1_g"][0], f).reshape(KC, 128).T)
    put("g2", np.asarray(inp["norm2_g"][0], f).reshape(KC, 128).T)
    gq = np.asarray(inp["q_norm_g"][0], f)
    gk = np.asarray(inp["k_norm_g"][0], f)
    put("gq", np.tile(gq, 2))
    put("gqp", np.tile(gq[p64], 2))
    put("gk", np.tile(gk, 2))
    put("gkp", np.tile(gk[p64], 2))
    put("ps", np.asarray(inp["pool_scale"][0], f).reshape(4, 128).T)
    cw = np.asarray(inp["conv_w"][0], f).reshape(3, 2 * NFT, 128).transpose(2, 0, 1)
    put("cw", cw.reshape(128, 132))
    put("cb", np.asarray(inp["conv_b"][0], f).reshape(2 * NFT, 128).T)
    invc = np.zeros((4, 16), f)
    for g, w in enumerate((2, 4, 8, 16)):
        for t in range(16):
            invc[g, t] = 1.0 / min(t + 1, w)
    put("invc", np.tile(invc.reshape(1, 64), (128, 1)))
    pw2 = np.array([2.0 ** (-k) for k in range(NITER + 2)], f)
    put("pw2", np.tile(pw2[None, :], (128, 1)))
    tq = np.arange(128)[:, None]
    sk = np.arange(128)[None, :]
    put("cmask", np.where(sk <= tq, 0.0, NEG).astype(f))
    sel = np.zeros((128, 64), f)
    sel[64 + np.arange(64), np.arange(64)] = 1.0
    sel[np.arange(64), np.arange(64)] = 1.0
    put("sel", sel)
    put("eps", np.full((128, 1), EPS, f))
    cstb = np.zeros((128, 384), f)
    cstb[:, 0:128] = np.eye(128, dtype=f)
    cstb[:, 128:256] = 1.0
    cstb[0:64, 256:320] = 1.0
    cstb[64:128, 320:384] = 1.0
    shared = dict(cst=cst, cstb=cstb, csq=_rope_tables(64, 2), csi=_rope_tables(32, 4), win=win, wvw=wvw,
                  poolw=poolw, wout=wout, wup=wup, wdn=wdn)
    x = np.asarray(inp["x"], f)
    in_maps = []
    for b in range(x.shape[0]):
        xTb = np.ascontiguousarray(x[b].T.reshape(KC, 128, T // 256, 256).transpose(1, 2, 0, 3))
        m = dict(shared)
        m["xT"] = xTb
        in_maps.append(m)
    return in_maps


_NC_CACHE = {}


def kernel(**inputs):
    in_maps = host_prepare(inputs)
    if "nc" not in _NC_CACHE:
        _NC_CACHE["nc"] = K().build()
    nc = _NC_CACHE["nc"]
    res = run_bass_kernel_spmd(nc, in_maps, core_ids=list(range(8)))
    outs = []
    for r in res.results:
        oT = np.asarray(r["outT"], np.float32)
        outs.append(oT.transpose(1, 0, 2).reshape(D, T).T)
    return np.ascontiguousarray(np.stack(outs, 0).astype(np.float32))
```
